# Optimizing a Trainium2 kernel written in Bass

```python
import math
import jax, jax.numpy as jnp
from jax import lax
import numpy as np

D_MODEL = 2048
BATCH = 16
SEQ = 256
DEPTH = 2
DEC_BATCH = 4
DEC_SEQ = 4096
PAST_LEN = 512

GRID_W = 64
N_EVEN = (DEPTH + 1) // 2
N_ODD = DEPTH // 2
H_A = 8
DH_A = 128
WIN_H = 8
WIN_W = 16
C_B = 1024
CONV_W = 31
H_C = 8
DH_C = 128
D_FF = 5632
N_SUB = 3
N_MOD = 3 * N_SUB
EPS = 1e-6
QBLOCK = 128
ROPE_BASE = 10000.0
W_A = H_A * DH_A
IN_A = 3 * W_A + 2 * C_B
OUT_A = W_A + C_B
IN_C = 2 * (2 * H_C * DH_C) + H_C * 2 * DH_C
OUT_C = H_C * 2 * DH_C

kernel_name = "hybrid_natten_conformer_diffattn_prefix_dit_step"


def _rms(x, g):
    xf = x.astype(jnp.float32)
    y = xf * lax.rsqrt(jnp.mean(xf * xf, axis=-1, keepdims=True) + EPS)
    return y.astype(x.dtype) * g


def _layernorm(x, g, b):
    xf = x.astype(jnp.float32)
    mu = jnp.mean(xf, axis=-1, keepdims=True)
    xc = xf - mu
    y = xc * lax.rsqrt(jnp.mean(xc * xc, axis=-1, keepdims=True) + EPS)
    return y.astype(x.dtype) * g + b


def _swiglu(h, w1, w3, w2):
    return (jax.nn.silu(h @ w1) * (h @ w3)) @ w2


def _adaln(x, m, i, g):
    return _rms(x, g) * (1.0 + m[:, None, 3 * i + 1]) + m[:, None, 3 * i]


def _gate(m, i):
    return m[:, None, 3 * i + 2]


def _heads(t, n_heads):
    b, n, _ = t.shape
    return t.reshape(b, n, n_heads, -1).transpose(0, 2, 1, 3)


def _merge(t):
    b, h, n, d = t.shape
    return t.transpose(0, 2, 1, 3).reshape(b, n, h * d)


def _query_blocks(fn, q):
    b, h, n, d = q.shape
    qb = q.reshape(b, h, n // QBLOCK, QBLOCK, d).transpose(2, 0, 1, 3, 4)
    out = lax.map(fn, qb)
    nb, _, ho, _, do = out.shape
    return out.transpose(1, 2, 0, 3, 4).reshape(b, ho, n, do)


def _attn_block(qb, k, v, scale):
    s = jnp.einsum('bhqd,bhkd->bhqk', qb, k).astype(jnp.float32) * scale
    p = jax.nn.softmax(s, axis=-1).astype(v.dtype)
    return jnp.einsum('bhqk,bhkd->bhqd', p, v)


def _diff_block(qb, k, v, lam, scale):
    b, h2, nq, _ = qb.shape
    s = jnp.einsum('bhqd,bhkd->bhqk', qb, k).astype(jnp.float32) * scale
    p = jax.nn.softmax(s, axis=-1).reshape(b, h2 // 2, 2, nq, -1)
    a = (p[:, :, 0] - lam * p[:, :, 1]).astype(v.dtype)
    return jnp.einsum('bhqk,bhkd->bhqd', a, v)


def _axial_rope(x):
    n, dh = x.shape[2], x.shape[3]
    t = jnp.arange(n)
    nf = dh // 4
    inv = ROPE_BASE ** (-jnp.arange(nf, dtype=jnp.float32) / nf)

    def rot(xh, pos):
        ang = pos.astype(jnp.float32)[:, None] * inv[None, :]
        cos = jnp.cos(ang).astype(x.dtype)
        sin = jnp.sin(ang).astype(x.dtype)
        x1, x2 = xh[..., :nf], xh[..., nf:]
        return jnp.concatenate([x1 * cos - x2 * sin, x2 * cos + x1 * sin], axis=-1)

    return jnp.concatenate([rot(x[..., : dh // 2], t // GRID_W), rot(x[..., dh // 2:], t % GRID_W)], axis=-1)


def _neighbourhood_attn(q, k, v, ck, cv, rpb):
    b, h, n, d = q.shape
    rows = n // GRID_W
    kh = min(WIN_H, rows)
    scale = d ** -0.5
    kg = k.reshape(b, h, rows, GRID_W, d)
    vg = v.reshape(b, h, rows, GRID_W, d)
    qg = q.reshape(b, h, rows, GRID_W, d).transpose(2, 0, 1, 3, 4)
    cols = jnp.arange(GRID_W)
    cs = jnp.clip(cols - WIN_W // 2, 0, GRID_W - WIN_W)
    col_mask = (cols[None, :] >= cs[:, None]) & (cols[None, :] < cs[:, None] + WIN_W)
    cidx = jnp.clip(cols[None, :] - cols[:, None] + WIN_W - 1, 0, 2 * WIN_W - 2)
    rpb_cols = rpb[:, :, cidx]

    def row_fn(args):
        r, qr = args
        rs = jnp.clip(r - kh // 2, 0, rows - kh)
        kb = lax.dynamic_slice_in_dim(kg, rs, kh, axis=2)
        vb = lax.dynamic_slice_in_dim(vg, rs, kh, axis=2)
        ridx = rs + jnp.arange(kh) - r + WIN_H - 1
        bias = rpb_cols[:, ridx].transpose(0, 2, 1, 3).astype(jnp.float32)
        s_lat = jnp.einsum('bhcd,bhjed->bhcje', qr, kb).astype(jnp.float32) * scale + bias[None]
        s_lat = jnp.where(col_mask[:, None, :], s_lat, -jnp.inf)
        s_ctx = jnp.einsum('bhcd,bhkd->bhck', qr, ck).astype(jnp.float32) * scale
        nc = s_ctx.shape[-1]
        s = jnp.concatenate([s_ctx, s_lat.reshape(b, h, GRID_W, kh * GRID_W)], axis=-1)
        p = jax.nn.softmax(s, axis=-1).astype(v.dtype)
        o_ctx = jnp.einsum('bhck,bhkd->bhcd', p[..., :nc], cv)
        o_lat = jnp.einsum('bhcje,bhjed->bhcd', p[..., nc:].reshape(b, h, GRID_W, kh, GRID_W), vb)
        return o_ctx + o_lat

    out = lax.map(row_fn, (jnp.arange(rows), qg))
    return out.transpose(1, 2, 0, 3, 4).reshape(b, h, n, d)


def _conv_module(u, dw_w, dw_b, ln_g, ln_b):
    a, gt = jnp.split(u, 2, axis=-1)
    z = a * jax.nn.sigmoid(gt)
    z = lax.conv_general_dilated(z, dw_w[:, None, :], window_strides=(1,),
                                 padding=((CONV_W // 2, CONV_W // 2),),
                                 dimension_numbers=('NWC', 'WIO', 'NWC'),
                                 feature_group_count=C_B) + dw_b
    return jax.nn.silu(_layernorm(z, ln_g, ln_b))


def _even_project(h, w_in):
    u = h @ w_in
    qa, ka, va, ub = jnp.split(u, [W_A, 2 * W_A, 3 * W_A], axis=-1)
    return _heads(qa, H_A), _heads(ka, H_A), _heads(va, H_A), ub


def _odd_project(h, w_in):
    u = h @ w_in
    q, k, v = jnp.split(u, [2 * H_C * DH_C, 4 * H_C * DH_C], axis=-1)
    return _heads(q, 2 * H_C), _heads(k, 2 * H_C), _heads(v, H_C)


def _diff_lambda(lp, lam_init):
    lf = lp.astype(jnp.float32)
    return jnp.exp(jnp.sum(lf[0] * lf[1])) - jnp.exp(jnp.sum(lf[2] * lf[3])) + lam_init


def _diff_out(o, subln_g, lam_init, w_out):
    return _merge(_rms(o, subln_g) * (1.0 - lam_init)) @ w_out


def setup_inputs(seed: int = 0) -> dict:
    key = jax.random.key(seed)
    ks = jax.random.split(key, 32)
    f32 = jnp.float32
    nrm = lambda k, shape, s: jax.random.normal(k, shape, f32) * s
    return {
        'x_prompt': nrm(ks[0], (BATCH, SEQ, D_MODEL), 1.0),
        'x_sample': nrm(ks[1], (DEC_BATCH, DEC_SEQ, D_MODEL), 1.0),
        'c': nrm(ks[2], (DEC_BATCH, D_MODEL), 1.0),
        'cache_a_k': nrm(ks[3], (DEC_BATCH, N_EVEN, H_A, PAST_LEN, DH_A), 1.0),
        'cache_a_v': nrm(ks[4], (DEC_BATCH, N_EVEN, H_A, PAST_LEN, DH_A), 1.0),
        'cache_c_k': nrm(ks[5], (DEC_BATCH, N_ODD, 2 * H_C, PAST_LEN, DH_C), 1.0),
        'cache_c_v': nrm(ks[6], (DEC_BATCH, N_ODD, H_C, PAST_LEN, 2 * DH_C), 1.0),
        'c_ctx': nrm(ks[7], (D_MODEL,), 1.0),
        'w_mod': nrm(ks[8], (DEPTH, D_MODEL, N_MOD * D_MODEL), 0.5 * D_MODEL ** -0.5),
        'b_mod': nrm(ks[9], (DEPTH, N_MOD * D_MODEL), 0.02),
        'norm_g': 1.0 + nrm(ks[10], (DEPTH, N_SUB, D_MODEL), 0.02),
        'ffn_w1': nrm(ks[11], (DEPTH, 2, D_MODEL, D_FF), D_MODEL ** -0.5),
        'ffn_w3': nrm(ks[12], (DEPTH, 2, D_MODEL, D_FF), D_MODEL ** -0.5),
        'ffn_w2': nrm(ks[13], (DEPTH, 2, D_FF, D_MODEL), D_FF ** -0.5),
        'a_w_in': nrm(ks[14], (N_EVEN, D_MODEL, IN_A), D_MODEL ** -0.5),
        'a_w_out': nrm(ks[15], (N_EVEN, OUT_A, D_MODEL), OUT_A ** -0.5),
        'a_rpb': nrm(ks[16], (N_EVEN, H_A, 2 * WIN_H - 1, 2 * WIN_W - 1), 0.1),
        'b_dw_w': nrm(ks[17], (N_EVEN, CONV_W, C_B), CONV_W ** -0.5),
        'b_dw_b': nrm(ks[18], (N_EVEN, C_B), 0.02),
        'b_ln_g': 1.0 + nrm(ks[19], (N_EVEN, C_B), 0.02),
        'b_ln_b': nrm(ks[20], (N_EVEN, C_B), 0.02),
        'c_w_in': nrm(ks[21], (N_ODD, D_MODEL, IN_C), D_MODEL ** -0.5),
        'c_w_out': nrm(ks[22], (N_ODD, OUT_C, D_MODEL), OUT_C ** -0.5),
        'c_lambda': nrm(ks[23], (N_ODD, 4, DH_C), 0.1),
        'c_subln_g': 1.0 + nrm(ks[24], (N_ODD, 2 * DH_C), 0.02),
        'final_g': 1.0 + nrm(ks[25], (D_MODEL,), 0.02),
    }


def reference(x_prompt, x_sample, c, cache_a_k, cache_a_v, cache_c_k, cache_c_v, c_ctx,
              w_mod, b_mod, norm_g, ffn_w1, ffn_w3, ffn_w2,
              a_w_in, a_w_out, a_rpb, b_dw_w, b_dw_b, b_ln_g, b_ln_b,
              c_w_in, c_w_out, c_lambda, c_subln_g, final_g):
    xp = x_prompt
    xs = x_sample
    new_a_k, new_a_v, new_c_k, new_c_v = [], [], [], []
    for l in range(DEPTH):
        m_ctx = (jax.nn.silu(c_ctx) @ w_mod[l] + b_mod[l]).reshape(1, N_MOD, D_MODEL)
        m_lat = (jax.nn.silu(c) @ w_mod[l] + b_mod[l]).reshape(-1, N_MOD, D_MODEL)
        g = norm_g[l]
        xp = xp + 0.5 * _gate(m_ctx, 0) * _swiglu(_adaln(xp, m_ctx, 0, g[0]), ffn_w1[l, 0], ffn_w3[l, 0], ffn_w2[l, 0])
        xs = xs + 0.5 * _gate(m_lat, 0) * _swiglu(_adaln(xs, m_lat, 0, g[0]), ffn_w1[l, 0], ffn_w3[l, 0], ffn_w2[l, 0])
        hp = _adaln(xp, m_ctx, 1, g[1])
        hs = _adaln(xs, m_lat, 1, g[1])
        if l % 2 == 0:
            e = l // 2
            conv_p = (b_dw_w[e], b_dw_b[e], b_ln_g[e], b_ln_b[e])
            q, k, v, ub = _even_project(hp, a_w_in[e])
            o_a = _query_blocks(lambda qb: _attn_block(qb, k, v, DH_A ** -0.5), q)
            yp = jnp.concatenate([_merge(o_a), _conv_module(ub, *conv_p)], axis=-1) @ a_w_out[e]
            new_a_k.append(k)
            new_a_v.append(v)
            q, k, v, ub = _even_project(hs, a_w_in[e])
            o_a = _neighbourhood_attn(q, k, v, cache_a_k[:, e], cache_a_v[:, e], a_rpb[e])
            ys = jnp.concatenate([_merge(o_a), _conv_module(ub, *conv_p)], axis=-1) @ a_w_out[e]
        else:
            o = l // 2
            lam_init = 0.8 - 0.6 * math.exp(-0.3 * l)
            lam = _diff_lambda(c_lambda[o], lam_init)
            sc = DH_C ** -0.5
            q, k, v = _odd_project(hp, c_w_in[o])
            op = _query_blocks(lambda qb: _diff_block(qb, k, v, lam, sc), q)
            yp = _diff_out(op, c_subln_g[o], lam_init, c_w_out[o])
            new_c_k.append(k)
            new_c_v.append(v)
            q, k, v = _odd_project(hs, c_w_in[o])
            q = _axial_rope(q)
            k_all = jnp.concatenate([cache_c_k[:, o], _axial_rope(k)], axis=2)
            v_all = jnp.concatenate([cache_c_v[:, o], v], axis=2)
            os_ = _query_blocks(lambda qb: _diff_block(qb, k_all, v_all, lam, sc), q)
            ys = _diff_out(os_, c_subln_g[o], lam_init, c_w_out[o])
        xp = xp + _gate(m_ctx, 1) * yp
        xs = xs + _gate(m_lat, 1) * ys
        xp = xp + 0.5 * _gate(m_ctx, 2) * _swiglu(_adaln(xp, m_ctx, 2, g[2]), ffn_w1[l, 1], ffn_w3[l, 1], ffn_w2[l, 1])
        xs = xs + 0.5 * _gate(m_lat, 2) * _swiglu(_adaln(xs, m_lat, 2, g[2]), ffn_w1[l, 1], ffn_w3[l, 1], ffn_w2[l, 1])
    y_prompt = _rms(xp, final_g)
    y_sample = _rms(xs, final_g)
    out_a_k = jnp.stack(new_a_k, axis=1)
    out_a_v = jnp.stack(new_a_v, axis=1)
    out_c_k = jnp.stack(new_c_k, axis=1)
    out_c_v = jnp.stack(new_c_v, axis=1)
    return (y_prompt, y_sample, out_a_k, out_a_v, out_c_k, out_c_v)
```

```python
import math
from contextlib import ExitStack
import numpy as np
import concourse.bass as bass
import concourse.mybir as mybir
from concourse.bass_utils import run_bass_kernel_spmd

F32 = mybir.dt.float32
BF16 = mybir.dt.bfloat16
AF = mybir.ActivationFunctionType
ALU = mybir.AluOpType
ENGS = ("pe", "act", "dve", "pool", "sp")

D = 2048
DFF = 5632
NKC = 16
NFC = 44
EPS = 1e-6
NTA = 3072
NTO = 2560


class Tok:
    __slots__ = ("lw", "rd")

    def __init__(self):
        self.lw = None
        self.rd = {}


class Op:
    __slots__ = ("eng", "fn", "deps", "signal", "is_dma", "dsem", "val", "cc")

    def __init__(self, eng, fn, is_dma, dsem):
        self.cc = False
        self.eng = eng
        self.fn = fn
        self.deps = []
        self.signal = is_dma
        self.is_dma = is_dma
        self.dsem = dsem
        self.val = None


class Prog:
    def __init__(self, nc):
        self.nc = nc
        self.ops = {e: [] for e in ENGS}
        self.last_dma = {}
        self.bar = None
        self.bar_seen = set()
        self.n = 0

    def barrier(self):
        deps = []
        for e in ENGS:
            for op in reversed(self.ops[e]):
                if not op.is_dma and op.fn is not None:
                    deps.append(op)
                    break
        deps.extend(self.last_dma.values())
        for d in deps:
            d.signal = True
        self.bar = deps
        self.bar_seen = set()

    def emit(self, eng, fn, reads=(), writes=(), dsem=None, cc=False):
        is_dma = dsem is not None
        op = Op(eng, fn, is_dma, dsem)
        op.cc = cc
        deps = {}
        for t in reads:
            if t.lw is not None:
                deps[id(t.lw)] = t.lw
        for t in writes:
            if t.lw is not None:
                deps[id(t.lw)] = t.lw
            for r in t.rd.values():
                deps[id(r)] = r
        if is_dma:
            p = self.last_dma.get(dsem)
            if p is not None:
                deps[id(p)] = p
            self.last_dma[dsem] = op
        if self.bar is not None and eng not in self.bar_seen:
            self.bar_seen.add(eng)
            for d in self.bar:
                deps[id(d)] = d
        for d in deps.values():
            if d is op:
                continue
            if (not is_dma) and eng == "pe" and d.eng == "pe" and not d.is_dma:
                continue
            d.signal = True
            op.deps.append(d)
        key = ("d", dsem) if is_dma else eng
        for t in reads:
            t.rd[key] = op
        for t in writes:
            t.lw = op
            t.rd = {}
        self.ops[eng].append(op)
        self.n += 1
        return op

    def build(self):
        nc = self.nc
        es = ExitStack()
        esem = {e: es.enter_context(nc.semaphore("s_" + e)) for e in ENGS}
        dsems = {}
        for e in ENGS:
            for op in self.ops[e]:
                if op.is_dma and op.dsem not in dsems:
                    dsems[op.dsem] = es.enter_context(nc.semaphore("d_%s" % (op.dsem,)))
        cnt = {e: 0 for e in ENGS}
        dcnt = {k: 0 for k in dsems}
        for e in ENGS:
            for op in self.ops[e]:
                if op.is_dma:
                    dcnt[op.dsem] += (1 if op.cc else 16)
                    op.val = (dsems[op.dsem], dcnt[op.dsem], ("d", op.dsem))
                elif op.signal and op.fn is not None:
                    cnt[e] += 1
                    op.val = (esem[e], cnt[e], e)
        self.stats = dict(cnt=cnt, n=self.n, nd=len(dsems), dmax=max(dcnt.values()) if dcnt else 0)
        handles = dict(pe="tensor", act="scalar", dve="vector", pool="gpsimd", sp="sync")
        with nc.Block() as block:
            for e in ENGS:
                ops = self.ops[e]

                def body(eng, ops=ops):
                    known = {}
                    for op in ops:
                        for d in op.deps:
                            if d.val is None:
                                continue
                            sem, val, key = d.val
                            if known.get(key, 0) >= val:
                                continue
                            known[key] = val
                            eng.wait_ge(sem, val)
                        if op.fn is None:
                            continue
                        ins = op.fn(eng)
                        if op.val is not None:
                            if op.cc:
                                ins.then_inc(op.val[0])
                            else:
                                ins.then_inc(op.val[0], 16 if op.is_dma else 1)

                getattr(block, handles[e])(body)
        es.close()


_UID = [0]


def _uid():
    _UID[0] += 1
    return "_u%d" % _UID[0]


def build_program(stop=99, dev=False, ncores=8):
    nc = bass.Bass("TRN2", target_bir_lowering=False)
    P = Prog(nc)

    def din(name, shape, dt=F32):
        return nc.dram_tensor(name, list(shape), dt, kind="ExternalInput").ap()

    def dout(name, shape, dt=F32):
        return nc.dram_tensor(name, list(shape), dt, kind="ExternalOutput").ap()

    def dscr(name, shape, dt, out=False):
        return nc.dram_tensor(name, list(shape), dt, kind="ExternalOutput" if out else "Internal").ap()

    xin = din("xin", [NTA, D])
    cT_d = din("cT", [128, 32])
    w_mod = din("w_mod_h", [2, D, 9 * 1024])
    b_mod = din("b_mod_h", [2, 9 * 1024])
    ag_in = dscr("ag_in", [36, 1024], F32)
    ag_out = dscr("ag_out", [72, 1024], F32)
    ngT_d = din("ngT", [128, 6 * 16])
    w1_d = din("ffn_w1", [2, 2, D, DFF])
    w3_d = din("ffn_w3", [2, 2, D, DFF])
    w2_d = din("ffn_w2", [2, 2, DFF, D])
    fgT_d = din("fgT", [128, 16])
    y_d = dout("y", [NTO, D])

    xT_s = dscr("xT_s", [128, NKC, NTA], F32, out=dev)
    tk_xTs = [[Tok() for _ in range(16)] for _ in range(6)]

    top = ExitStack()

    def sb(name, shape, dt):
        return top.enter_context(nc.sbuf_tensor(name, list(shape), dt))

    ident = sb("ident", [128, 128], F32)
    ones_bf = sb("ones_bf", [128, 128], BF16)
    modA = sb("modA", [128, 2, 3, 16, 2], F32)
    modB = sb("modB", [128, 2, 3, 16, 2], F32)
    modG = sb("modG", [128, 2, 3, 16, 2], F32)
    fgT = sb("fgT_sb", [128, 16], F32)
    t_const = Tok()
    t_mod = Tok()
    PS = [top.enter_context(nc.psum_tensor("ps%d" % i, [128, 512], F32)) for i in range(8)]
    PT = [Tok() for _ in range(8)]

    ident_d = din("ident_in", [128, 128])
    P.emit("sp", lambda e: e.dma_start(out=ident[:], in_=ident_d), writes=[t_const], dsem="c0")
    P.emit("dve", lambda e: e.memset(ones_bf[:], 1.0), writes=[t_const])
    P.emit("sp", lambda e: e.dma_start(out=fgT[:], in_=fgT_d), writes=[t_const], dsem="c0")

    ph_s0 = ExitStack()
    if True:
        def psb(name, shape, dt, ph=ph_s0):
            return ph.enter_context(nc.sbuf_tensor(name + _uid(), list(shape), dt))
        cT = psb("cT_sb", [128, 32], F32)
        sT = psb("sT_sb", [128, 32], BF16)
        ngT = psb("ngT_sb", [128, 96], F32)
        wm = [psb("wm%d" % i, [128, 2048], BF16) for i in range(4)]
        t_wm = [Tok() for _ in range(4)]
        bm = [psb("bm%d" % i, [2, 2048], F32) for i in range(2)]
        t_bm = [Tok() for _ in range(2)]
        mrow = [psb("mrow%d" % i, [2, 2048], F32) for i in range(2)]
        t_mrow = [Tok() for _ in range(2)]
        modT = psb("modT", [128, 2, 9, 16, 2], F32)
        t_c = Tok()
        P.emit("sp", lambda e: e.dma_start(out=cT[:], in_=cT_d), writes=[t_c], dsem="c1")
        P.emit("sp", lambda e: e.dma_start(out=ngT[:], in_=ngT_d), writes=[t_c], dsem="c1")
        P.emit("act", lambda e: e.activation(out=sT[:], in_=cT[:], func=AF.Silu), reads=[t_c], writes=[t_c])
        sTv = sT[:].rearrange("p (c n) -> p c n", n=2)
        t_agi, t_ago = Tok(), Tok()

        def s0_units():
            it = 0
            for l in range(2):
                for j0 in range(0, 9, 2):
                    js = [j for j in (j0, j0 + 1) if j < 9]
                    width = 1024 * len(js)
                    for j in js:
                        b = j % 2
                        P.emit("sp", lambda e, l=l, j=j, b=b: e.dma_start(
                            out=bm[b][:, 0:1024], in_=b_mod[l:l + 1, j * 1024:(j + 1) * 1024].to_broadcast([2, 1024])),
                            writes=[t_bm[b]], dsem="bm%d" % b)
                    for kc in range(16):
                        s = it % 4
                        it += 1
                        P.emit("pool", lambda e, l=l, j0=j0, kc=kc, s=s, width=width: e.dma_start(
                            out=wm[s][:, 0:width], in_=w_mod[l, kc * 128:(kc + 1) * 128, j0 * 1024:j0 * 1024 + width]),
                            writes=[t_wm[s]], dsem="wm%d" % s)
                        for nb in range(width // 512):
                            P.emit("pe", lambda e, kc=kc, s=s, nb=nb: e.matmul(
                                PS[nb][0:2, :], lhsT=sTv[:, kc, :], rhs=wm[s][:, nb * 512:(nb + 1) * 512],
                                start=(kc == 0), stop=(kc == 15)),
                                reads=[t_c, t_wm[s]], writes=[PT[nb]])
                        yield
                    for ji, j in enumerate(js):
                        b = j % 2
                        for nb in range(2):
                            P.emit("dve", lambda e, b=b, nb=nb, ji=ji: e.tensor_tensor(
                                out=mrow[b][:, nb * 512:(nb + 1) * 512], in0=PS[2 * ji + nb][0:2, :],
                                in1=bm[b][:, nb * 512:(nb + 1) * 512], op=ALU.add),
                                reads=[PT[2 * ji + nb], t_bm[b]], writes=[t_mrow[b]])
                        r0 = (l * 9 + j) * 2
                        P.emit("sp", lambda e, b=b, r0=r0: e.dma_start(out=ag_in[r0:r0 + 2, :], in_=mrow[b][:, 0:1024]),
                               reads=[t_mrow[b]], writes=[t_agi], dsem="ag_s")

        s0_gen = s0_units()

    ph_s1 = ExitStack()
    if True:
        ph = ph_s1

        def psb(name, shape, dt):
            return ph.enter_context(nc.sbuf_tensor(name + _uid(), list(shape), dt))
        xt = [psb("xt%d" % i, [128, D], F32) for i in range(2)]
        t_xt = [Tok() for _ in range(2)]
        st = [psb("st%d" % i, [128, 16, 512], F32) for i in range(2)]
        t_st = [Tok() for _ in range(2)]
        for blk in range(6):
            sb_ = blk % 2
            for ti in range(4):
                t = blk * 4 + ti
                xb = t % 2
                P.emit("sp", lambda e, t=t, xb=xb: e.dma_start(out=xt[xb][:], in_=xin[t * 128:(t + 1) * 128, :]),
                       writes=[t_xt[xb]], dsem="xt%d" % xb)
                for g in range(4):
                    for c4 in range(4):
                        c = g * 4 + c4
                        P.emit("pe", lambda e, xb=xb, c=c, g=g, c4=c4: e.transpose(
                            out=PS[4 + g][:, c4 * 128:(c4 + 1) * 128], in_=xt[xb][:, c * 128:(c + 1) * 128],
                            identity=ident[:]),
                            reads=[t_xt[xb], t_const], writes=[PT[4 + g]])
                    eng = "dve" if g % 2 == 0 else "act"
                    if eng == "dve":
                        P.emit("dve", lambda e, g=g, sb_=sb_, ti=ti: e.tensor_copy(
                            out=st[sb_][:, g * 4:(g + 1) * 4, ti * 128:(ti + 1) * 128],
                            in_=PS[4 + g][:].rearrange("p (c t) -> p c t", c=4)),
                            reads=[PT[4 + g]], writes=[t_st[sb_]])
                    else:
                        P.emit("act", lambda e, g=g, sb_=sb_, ti=ti: e.activation(
                            out=st[sb_][:, g * 4:(g + 1) * 4, ti * 128:(ti + 1) * 128],
                            in_=PS[4 + g][:].rearrange("p (c t) -> p c t", c=4), func=AF.Copy),
                            reads=[PT[4 + g]], writes=[t_st[sb_]])
            for _ in range(28):
                next(s0_gen, None)
            P.emit("sp", lambda e, blk=blk, sb_=sb_: e.dma_start(
                out=xT_s[:, :, blk * 512:(blk + 1) * 512], in_=st[sb_][:]),
                reads=[t_st[sb_]], writes=tk_xTs[blk], dsem="st%d" % sb_)
        for _ in s0_gen:
            pass

        P.emit("pool", lambda e: e.collective_compute("AllGather", ALU.bypass,
                                                      replica_groups=[[2 * i, 2 * i + 1] for i in range(ncores // 2)],
                                                      ins=[ag_in], outs=[ag_out]),
               reads=[t_agi], writes=[t_ago], dsem="cc_m", cc=True)
        for l in range(2):
            for j in range(9):
                b = (l * 9 + j) % 2
                r0 = (l * 9 + j) * 2
                for r in range(2):
                    P.emit("sp", lambda e, b=b, r0=r0, r=r: e.dma_start(
                        out=mrow[b][:, r * 1024:(r + 1) * 1024], in_=ag_out[r * 36 + r0:r * 36 + r0 + 2, :]),
                        reads=[t_ago], writes=[t_mrow[b]], dsem="ag_l%d" % b)
                for c in range(16):
                    P.emit("pe", lambda e, b=b, c=c: e.matmul(
                        PS[4][:, c * 2:c * 2 + 2], lhsT=mrow[b][0:2, c * 128:(c + 1) * 128], rhs=ident[0:2, 0:2],
                        start=True, stop=True),
                        reads=[t_mrow[b], t_const], writes=[PT[4]])
                P.emit("dve", lambda e, l=l, j=j: e.tensor_copy(
                    out=modT[:, l, j].rearrange("p c n -> p (c n)"), in_=PS[4][:, 0:32]),
                    reads=[PT[4]], writes=[t_mod])
        for l in range(2):
            for i in range(3):
                for cnd in range(2):
                    P.emit("dve", lambda e, l=l, i=i, cnd=cnd: e.scalar_tensor_tensor(
                        out=modA[:, l, i, :, cnd], in0=modT[:, l, 3 * i + 1, :, cnd], scalar=1.0,
                        in1=ngT[:, (l * 3 + i) * 16:(l * 3 + i + 1) * 16], op0=ALU.add, op1=ALU.mult),
                        reads=[t_mod, t_c], writes=[t_mod])
                P.emit("dve", lambda e, l=l, i=i: e.tensor_copy(
                    out=modB[:, l, i].rearrange("p c n -> p (c n)"),
                    in_=modT[:, l, 3 * i].rearrange("p c n -> p (c n)")), reads=[t_mod], writes=[t_mod])
                P.emit("dve", lambda e, l=l, i=i: e.tensor_scalar(
                    out=modG[:, l, i].rearrange("p c n -> p (c n)"),
                    in0=modT[:, l, 3 * i + 2].rearrange("p c n -> p (c n)"),
                    scalar1=(1.0 if i == 1 else 0.5), scalar2=None, op0=ALU.mult),
                    reads=[t_mod], writes=[t_mod])
    P.barrier()
    ph_s1.close()
    ph_s0.close()

    def ffn_phase(l, half, nblocks):
        sub = 0 if half == 0 else 2
        w1 = w1_d[l, half]
        w3 = w3_d[l, half]
        w2 = w2_d[l, half].rearrange("(k p) n -> p k n", p=128)
        w1r = w1.rearrange("(k p) n -> p k n", p=128)
        w3r = w3.rearrange("(k p) n -> p k n", p=128)
        with ExitStack() as ph:
            def psb(name, shape, dt):
                return ph.enter_context(nc.sbuf_tensor(name + _uid(), list(shape), dt))
            xT = psb("f_xT", [128, 16, 512], F32)
            t_xT = [Tok() for _ in range(16)]
            hT2 = [psb("f_hT", [128, 16, 512], BF16) for _ in range(2)]
            t_hT2 = [[Tok() for _ in range(16)] for _ in range(2)]
            gT = psb("f_gT", [128, NFC, 512], BF16)
            t_gT = [Tok() for _ in range(NFC)]
            arena = psb("f_w", [128, 4 * 8192], BF16)
            t_w = [Tok() for _ in range(4)]
            lru = [0, 0, 0, 0]
            clock = [0]
            bufs = norm_bufs(psb, "f_")
            sil = [psb("f_sil%d" % i, [128, 512], F32) for i in range(2)]
            t_sil = [Tok() for _ in range(2)]
            xc = [psb("f_xc%d" % i, [128, 512], F32) for i in range(3)]
            t_xc = [Tok() for _ in range(3)]

            def wslot1():
                r = min(range(4), key=lambda i: lru[i])
                clock[0] += 1
                lru[r] = clock[0]
                return r

            def wslot2():
                pr = 0 if max(lru[0], lru[1]) <= max(lru[2], lru[3]) else 1
                clock[0] += 1
                lru[2 * pr] = lru[2 * pr + 1] = clock[0]
                return pr

            norm_block(psb, 0, l, sub, xT, t_xT, hT2[0], t_hT2[0], bufs)
            xi = 0
            for b in range(nblocks):
                cnd = 0 if b == 0 else 1
                G = modG[:, l, sub, :, cnd]
                hT, t_hT = hT2[b % 2], t_hT2[b % 2]
                for fg in range(NFC // 4):
                    s1 = wslot1()
                    s3 = wslot1()
                    v1 = arena[:, s1 * 8192:(s1 + 1) * 8192].rearrange("p (k n) -> p k n", k=16)
                    v3 = arena[:, s3 * 8192:(s3 + 1) * 8192].rearrange("p (k n) -> p k n", k=16)
                    P.emit("pool", lambda e, v1=v1, fg=fg: e.dma_start(out=v1, in_=w1r[:, :, fg * 512:(fg + 1) * 512]),
                           writes=[t_w[s1]], dsem="w%d" % s1)
                    P.emit("pool", lambda e, v3=v3, fg=fg: e.dma_start(out=v3, in_=w3r[:, :, fg * 512:(fg + 1) * 512]),
                           writes=[t_w[s3]], dsem="w%d" % s3)
                    for j in range(4):
                        fc = fg * 4 + j
                        p1 = fc % 2
                        p3 = 2 + fc % 2
                        for kc in range(16):
                            P.emit("pe", lambda e, v1=v1, j=j, kc=kc, p1=p1, hT=hT: e.matmul(
                                PS[p1][:], lhsT=v1[:, kc, j * 128:(j + 1) * 128], rhs=hT[:, kc, :],
                                start=(kc == 0), stop=(kc == 15)),
                                reads=[t_w[s1], t_hT[kc]], writes=[PT[p1]])
                        for kc in range(16):
                            P.emit("pe", lambda e, v3=v3, j=j, kc=kc, p3=p3, hT=hT: e.matmul(
                                PS[p3][:], lhsT=v3[:, kc, j * 128:(j + 1) * 128], rhs=hT[:, kc, :],
                                start=(kc == 0), stop=(kc == 15)),
                                reads=[t_w[s3], t_hT[kc]], writes=[PT[p3]])
                        P.emit("act", lambda e, fc=fc, p1=p1: e.activation(out=sil[fc % 2][:], in_=PS[p1][:], func=AF.Silu),
                               reads=[PT[p1]], writes=[t_sil[fc % 2]])
                        P.emit("dve", lambda e, fc=fc, p3=p3: e.tensor_tensor(
                            out=gT[:, fc, :], in0=sil[fc % 2][:], in1=PS[p3][:], op=ALU.mult),
                            reads=[t_sil[fc % 2], PT[p3]], writes=[t_gT[fc]])
                    if fg == 3 and b + 1 < nblocks:
                        norm_block(psb, b + 1, l, sub, xT, t_xT, hT2[(b + 1) % 2], t_hT2[(b + 1) % 2], bufs)
                for ng in range(8):
                    pr = wslot2()
                    v2 = arena[:, pr * 16384:pr * 16384 + NFC * 256].rearrange("p (k n) -> p k n", k=NFC)
                    P.emit("pool", lambda e, v2=v2, ng=ng: e.dma_start(out=v2, in_=w2[:, :, ng * 256:(ng + 1) * 256]),
                           writes=[t_w[2 * pr], t_w[2 * pr + 1]], dsem="w%d" % (2 * pr))
                    for jj in range(2):
                        n_ = ng * 2 + jj
                        py = 4 + n_ % 2
                        k = xi % 3
                        xi += 1
                        P.emit("sp", lambda e, k=k, n_=n_, b=b: e.dma_start(out=xc[k][:], in_=xT_s[:, n_, b * 512:(b + 1) * 512]),
                               reads=[tk_xTs[b][n_]], writes=[t_xc[k]], dsem="f_c%d" % k)
                        for kc in range(NFC):
                            P.emit("pe", lambda e, v2=v2, jj=jj, kc=kc, py=py: e.matmul(
                                PS[py][:], lhsT=v2[:, kc, jj * 128:(jj + 1) * 128], rhs=gT[:, kc, :],
                                start=(kc == 0), stop=(kc == NFC - 1)),
                                reads=[t_w[2 * pr], t_w[2 * pr + 1], t_gT[kc]], writes=[PT[py]])
                        P.emit("dve", lambda e, n_=n_, py=py, G=G, k=k: e.scalar_tensor_tensor(
                            out=xc[k][:], in0=PS[py][:], scalar=G[:, n_:n_ + 1], in1=xc[k][:],
                            op0=ALU.mult, op1=ALU.add),
                            reads=[PT[py], t_xc[k], t_mod], writes=[t_xc[k]])
                        P.emit("sp", lambda e, k=k, n_=n_, b=b: e.dma_start(out=xT_s[:, n_, b * 512:(b + 1) * 512], in_=xc[k][:]),
                               reads=[t_xc[k]], writes=[tk_xTs[b][n_]], dsem="f_c%d" % k)
        P.barrier()

    def final_phase():
        with ExitStack() as ph:
            def psb(name, shape, dt):
                return ph.enter_context(nc.sbuf_tensor(name + _uid(), list(shape), dt))
            xT = psb("z_xT", [128, 16, 512], F32)
            t_xT = Tok()
            sq = [psb("z_sq%d" % i, [128, 512], BF16) for i in range(2)]
            t_sq = [Tok() for _ in range(2)]
            rstd = psb("z_rstd", [128, 512], F32)
            t_rstd = Tok()
            yt = [psb("z_yt%d" % i, [128, D], F32) for i in range(2)]
            t_yt = [Tok() for _ in range(2)]
            for b in range(5):
                P.emit("sp", lambda e, b=b: e.dma_start(out=xT[:], in_=xT_s[:, :, b * 512:(b + 1) * 512]),
                       reads=tk_xTs[b], writes=[t_xT], dsem="z_x")
                for c in range(16):
                    P.emit("act", lambda e, c=c: e.activation(out=sq[c % 2][:], in_=xT[:, c, :], func=AF.Square),
                           reads=[t_xT], writes=[t_sq[c % 2]])
                    P.emit("pe", lambda e, c=c: e.matmul(PS[6][:], lhsT=ones_bf[:], rhs=sq[c % 2][:],
                                                         start=(c == 0), stop=(c == 15)),
                           reads=[t_sq[c % 2], t_const], writes=[PT[6]])
                P.emit("dve", lambda e: e.tensor_scalar(out=rstd[:], in0=PS[6][:], scalar1=1.0 / D, scalar2=EPS,
                                                        op0=ALU.mult, op1=ALU.add), reads=[PT[6]], writes=[t_rstd])
                P.emit("act", lambda e: e.activation(out=rstd[:], in_=rstd[:], func=AF.Sqrt),
                       reads=[t_rstd], writes=[t_rstd])
                P.emit("dve", lambda e: e.reciprocal(out=rstd[:], in_=rstd[:]), reads=[t_rstd], writes=[t_rstd])
                for c in range(16):
                    P.emit("dve", lambda e, c=c: e.scalar_tensor_tensor(
                        out=xT[:, c, :], in0=xT[:, c, :], scalar=fgT[:, c:c + 1], in1=rstd[:],
                        op0=ALU.mult, op1=ALU.mult),
                        reads=[t_xT, t_rstd, t_const], writes=[t_xT])
                for ti in range(4):
                    yb = ti % 2
                    for g in range(4):
                        for c4 in range(4):
                            c = g * 4 + c4
                            P.emit("pe", lambda e, c=c, g=g, c4=c4, ti=ti: e.transpose(
                                out=PS[g][:, c4 * 128:(c4 + 1) * 128], in_=xT[:, c, ti * 128:(ti + 1) * 128],
                                identity=ident[:]),
                                reads=[t_xT, t_const], writes=[PT[g]])
                        if g % 2 == 0:
                            P.emit("dve", lambda e, g=g, yb=yb: e.tensor_copy(
                                out=yt[yb][:, g * 512:(g + 1) * 512], in_=PS[g][:]),
                                reads=[PT[g]], writes=[t_yt[yb]])
                        else:
                            P.emit("act", lambda e, g=g, yb=yb: e.activation(
                                out=yt[yb][:, g * 512:(g + 1) * 512], in_=PS[g][:], func=AF.Copy),
                                reads=[PT[g]], writes=[t_yt[yb]])
                    t0 = b * 512 + ti * 128
                    P.emit("sp", lambda e, t0=t0, yb=yb: e.dma_start(out=y_d[t0:t0 + 128, :], in_=yt[yb][:]),
                           reads=[t_yt[yb]], dsem="z_y%d" % yb)
        P.barrier()

    def norm_block(psb, b, l, sub, xT, t_xT, hT, t_hT, bufs):
        cnd = 0 if b == 0 else 1
        A = modA[:, l, sub, :, cnd]
        B = modB[:, l, sub, :, cnd]
        sq, t_sq, rstd, t_rstd, tmp, t_tmp = bufs
        P.emit("sp", lambda e, b=b: e.dma_start(out=xT[:], in_=xT_s[:, :, b * 512:(b + 1) * 512]),
               reads=tk_xTs[b], writes=t_xT, dsem="n_x")
        for c in range(16):
            P.emit("act", lambda e, c=c: e.activation(out=sq[c % 2][:], in_=xT[:, c, :], func=AF.Square),
                   reads=[t_xT[c]], writes=[t_sq[c % 2]])
            P.emit("pe", lambda e, c=c: e.matmul(PS[6][:], lhsT=ones_bf[:], rhs=sq[c % 2][:],
                                                 start=(c == 0), stop=(c == 15)),
                   reads=[t_sq[c % 2], t_const], writes=[PT[6]])
        P.emit("dve", lambda e: e.tensor_scalar(out=rstd[:], in0=PS[6][:], scalar1=1.0 / D, scalar2=EPS,
                                                op0=ALU.mult, op1=ALU.add), reads=[PT[6]], writes=[t_rstd])
        P.emit("act", lambda e: e.activation(out=rstd[:], in_=rstd[:], func=AF.Sqrt),
               reads=[t_rstd], writes=[t_rstd])
        P.emit("dve", lambda e: e.reciprocal(out=rstd[:], in_=rstd[:]), reads=[t_rstd], writes=[t_rstd])
        for c in range(16):
            P.emit("dve", lambda e, c=c: e.scalar_tensor_tensor(
                out=tmp[c % 2][:], in0=xT[:, c, :], scalar=A[:, c:c + 1], in1=rstd[:],
                op0=ALU.mult, op1=ALU.mult),
                reads=[t_xT[c], t_rstd, t_mod], writes=[t_tmp[c % 2]])
            P.emit("act", lambda e, c=c: e.activation(
                out=hT[:, c, :], in_=tmp[c % 2][:], func=AF.Identity, bias=B[:, c:c + 1], scale=1.0),
                reads=[t_tmp[c % 2], t_mod], writes=[t_hT[c]])

    def norm_bufs(psb, pfx):
        sq = [psb(pfx + "sq%d" % i, [128, 512], BF16) for i in range(2)]
        rstd = psb(pfx + "rstd", [128, 512], F32)
        tmp = [psb(pfx + "tmp%d" % i, [128, 512], F32) for i in range(2)]
        return (sq, [Tok(), Tok()], rstd, Tok(), tmp, [Tok(), Tok()])

    def wout_phase(l, w_out_d, mixT_s, tk_mix):
        wr = w_out_d.rearrange("(k p) n -> p k n", p=128)
        with ExitStack() as ph:
            def psb(name, shape, dt):
                return ph.enter_context(nc.sbuf_tensor(name + _uid(), list(shape), dt))
            wo = psb("o_w", [128, 16, D], BF16)
            t_wo = Tok()
            xT = psb("o_xT", [128, 16, 512], F32)
            t_xT = [Tok() for _ in range(16)]
            mx = [psb("o_mx%d" % i, [128, 16, 512], BF16) for i in range(2)]
            t_mx = [Tok() for _ in range(2)]
            for g in range(4):
                P.emit("pool", lambda e, g=g: e.dma_start(out=wo[:, :, g * 512:(g + 1) * 512],
                                                          in_=wr[:, :, g * 512:(g + 1) * 512]),
                       writes=[t_wo], dsem="o_w")
            for b in range(5):
                cnd = 0 if b == 0 else 1
                G = modG[:, l, 1, :, cnd]
                m = b % 2
                P.emit("sp", lambda e, b=b, m=m: e.dma_start(
                    out=mx[m][:], in_=mixT_s[:, :, b * 512:(b + 1) * 512].rearrange("c p t -> p c t")),
                    reads=[tk_mix], writes=[t_mx[m]], dsem="o_m%d" % m)
                P.emit("sp", lambda e, b=b: e.dma_start(out=xT[:], in_=xT_s[:, :, b * 512:(b + 1) * 512]),
                       reads=tk_xTs[b], writes=t_xT, dsem="o_x")
                for n_ in range(16):
                    py = n_ % 4
                    for kc in range(16):
                        P.emit("pe", lambda e, n_=n_, kc=kc, py=py, m=m: e.matmul(
                            PS[py][:], lhsT=wo[:, kc, n_ * 128:(n_ + 1) * 128], rhs=mx[m][:, kc, :],
                            start=(kc == 0), stop=(kc == 15)),
                            reads=[t_wo, t_mx[m]], writes=[PT[py]])
                    P.emit("dve", lambda e, n_=n_, py=py, G=G: e.scalar_tensor_tensor(
                        out=xT[:, n_, :], in0=PS[py][:], scalar=G[:, n_:n_ + 1], in1=xT[:, n_, :],
                        op0=ALU.mult, op1=ALU.add),
                        reads=[PT[py], t_xT[n_], t_mod], writes=[t_xT[n_]])
                P.emit("sp", lambda e, b=b: e.dma_start(out=xT_s[:, :, b * 512:(b + 1) * 512], in_=xT[:]),
                       reads=t_xT, writes=tk_xTs[b], dsem="o_xs")
        P.barrier()

    a_w_in = din("a_w_in", [D, 5120])
    a_w_out = din("a_w_out", [D, D])
    nb_d = din("nbias", [8, 128, 5 * 768])
    ck_a = din("cache_a_k", [8, 512, 128])
    cv_a = din("cache_a_v", [8, 512, 128])
    dwT_d = din("dwT", [128, 8 * 31])
    cvp_d = din("convp", [128, 24])
    zmask_d = din("zmask", [1, 512])
    nak_d = dout("nak", [512, 1024])
    nav_d = dout("nav", [512, 1024])
    qT_s = dscr("qT_s", [8, 128, NTO], BF16)
    kT_s = dscr("kT_s", [8, 128, NTA], BF16)
    v_s = dscr("v_s", [NTA, 1024], BF16)
    zT_s = dscr("zT_s", [8, 128, NTA], BF16)
    mixT_s = dscr("mixT_s", [16, 128, NTO], BF16, out=dev)
    tk_q, tk_k, tk_v, tk_z, tk_mix = Tok(), Tok(), Tok(), Tok(), Tok()
    SC_A = 128 ** -0.5

    def even_proj():
        wr = a_w_in.rearrange("(k p) n -> p k n", p=128)
        with ExitStack() as ph:
            def psb(name, shape, dt):
                return ph.enter_context(nc.sbuf_tensor(name + _uid(), list(shape), dt))
            xT = psb("e_xT", [128, 16, 512], F32)
            t_xT = [Tok() for _ in range(16)]
            hT2 = [psb("e_hT", [128, 16, 512], BF16) for _ in range(2)]
            t_hT2 = [[Tok() for _ in range(16)] for _ in range(2)]
            ch_ = {}
            bufs = norm_bufs(psb, "e_")
            wsl = [psb("e_w%d" % i, [128, 16, 512], BF16) for i in range(3)]
            t_w = [Tok() for _ in range(3)]
            stg = [psb("e_stg%d" % i, [128, 512], BF16) for i in range(4)]
            t_stg = [Tok() for _ in range(4)]
            stf = [psb("e_stf%d" % i, [128, 512], F32) for i in range(2)]
            t_stf = [Tok() for _ in range(2)]
            a_st = psb("e_ast", [128, 4, 512], F32)
            t_ast = Tok()
            sig = [psb("e_sig%d" % i, [128, 512], F32) for i in range(2)]
            t_sig = [Tok() for _ in range(2)]
            zm = psb("e_zm", [128, 512], F32)
            t_zm = Tok()
            P.emit("sp", lambda e: e.dma_start(out=zm[:], in_=zmask_d.to_broadcast([128, 512])),
                   writes=[t_zm], dsem="e_zm")
            cnt = [0, 0, 0, 0]

            def loadw(g):
                s = cnt[0] % 3
                cnt[0] += 1
                P.emit("pool", lambda e, s=s, g=g: e.dma_start(out=wsl[s][:], in_=wr[:, :, g * 512:(g + 1) * 512]),
                       writes=[t_w[s]], dsem="e_w%d" % s)
                return s

            def featmajor(s, j):
                py = cnt[1] % 4
                cnt[1] += 1
                hT, t_hT = ch_["hT"], ch_["t_hT"]
                for kc in range(16):
                    P.emit("pe", lambda e, s=s, j=j, kc=kc, py=py, hT=hT: e.matmul(
                        PS[py][:], lhsT=wsl[s][:, kc, j * 128:(j + 1) * 128], rhs=hT[:, kc, :],
                        start=(kc == 0), stop=(kc == 15)),
                        reads=[t_w[s], t_hT[kc]], writes=[PT[py]])
                return py

            def tokmajor(s, ti):
                py = cnt[1] % 4
                cnt[1] += 1
                hT, t_hT = ch_["hT"], ch_["t_hT"]
                for kc in range(16):
                    P.emit("pe", lambda e, s=s, ti=ti, kc=kc, py=py, hT=hT: e.matmul(
                        PS[py][:], lhsT=hT[:, kc, ti * 128:(ti + 1) * 128], rhs=wsl[s][:, kc, :],
                        start=(kc == 0), stop=(kc == 15)),
                        reads=[t_w[s], t_hT[kc]], writes=[PT[py]])
                return py

            def to_bf(py, eng):
                i = cnt[2] % 4
                cnt[2] += 1
                if eng == "act":
                    P.emit("act", lambda e, i=i, py=py: e.activation(out=stg[i][:], in_=PS[py][:], func=AF.Copy),
                           reads=[PT[py]], writes=[t_stg[i]])
                else:
                    P.emit("dve", lambda e, i=i, py=py: e.tensor_copy(out=stg[i][:], in_=PS[py][:]),
                           reads=[PT[py]], writes=[t_stg[i]])
                return i

            norm_block(psb, 0, 0, 1, xT, t_xT, hT2[0], t_hT2[0], bufs)
            for b in range(6):
                ch_["hT"], ch_["t_hT"] = hT2[b % 2], t_hT2[b % 2]
                t0 = b * 512
                for g in range(6):
                    if g == 3 and b + 1 < 6:
                        norm_block(psb, b + 1, 0, 1, xT, t_xT, hT2[(b + 1) % 2], t_hT2[(b + 1) % 2], bufs)
                    if g < 2 and b == 5:
                        continue
                    s = loadw(g)
                    if g < 4:
                        for j in range(4):
                            h = (g % 2) * 4 + j
                            py = featmajor(s, j)
                            i = to_bf(py, "act" if j % 2 else "dve")
                            if g < 2:
                                P.emit("sp", lambda e, i=i, h=h, t0=t0: e.dma_start(out=qT_s[h, :, t0:t0 + 512], in_=stg[i][:]),
                                       reads=[t_stg[i]], writes=[tk_q], dsem="e_s%d" % i)
                            else:
                                P.emit("sp", lambda e, i=i, h=h, t0=t0: e.dma_start(out=kT_s[h, :, t0:t0 + 512], in_=stg[i][:]),
                                       reads=[t_stg[i]], writes=[tk_k], dsem="e_s%d" % i)
                    if (g in (2, 3) and b == 0) or g in (4, 5):
                        for ti in range(4):
                            py = tokmajor(s, ti)
                            r0 = t0 + ti * 128
                            if b == 0:
                                f = cnt[3] % 2
                                cnt[3] += 1
                                P.emit("dve", lambda e, f=f, py=py: e.tensor_copy(out=stf[f][:], in_=PS[py][:]),
                                       reads=[PT[py]], writes=[t_stf[f]])
                                od = nak_d if g < 4 else nav_d
                                c0 = (g % 2) * 512
                                P.emit("sp", lambda e, f=f, od=od, r0=r0, c0=c0: e.dma_start(
                                    out=od[r0:r0 + 128, c0:c0 + 512], in_=stf[f][:]),
                                    reads=[t_stf[f]], dsem="e_f%d" % f)
                            if g in (4, 5):
                                if b == 0:
                                    i = cnt[2] % 4
                                    cnt[2] += 1
                                    P.emit("act", lambda e, i=i, f=f: e.activation(out=stg[i][:], in_=stf[f][:], func=AF.Copy),
                                           reads=[t_stf[f]], writes=[t_stg[i]])
                                else:
                                    i = to_bf(py, "act" if ti % 2 else "dve")
                                c0 = (g - 4) * 512
                                P.emit("sp", lambda e, i=i, r0=r0, c0=c0: e.dma_start(
                                    out=v_s[r0:r0 + 128, c0:c0 + 512], in_=stg[i][:]),
                                    reads=[t_stg[i]], writes=[tk_v], dsem="e_s%d" % i)
                for hf in range(2):
                    s = loadw(6 + hf)
                    for j in range(4):
                        py = featmajor(s, j)
                        P.emit("dve", lambda e, j=j, py=py: e.tensor_copy(out=a_st[:, j, :], in_=PS[py][:]),
                               reads=[PT[py]], writes=[t_ast])
                    s = loadw(8 + hf)
                    for j in range(4):
                        ch = hf * 4 + j
                        py = featmajor(s, j)
                        P.emit("act", lambda e, j=j, py=py: e.activation(out=sig[j % 2][:], in_=PS[py][:], func=AF.Sigmoid),
                               reads=[PT[py]], writes=[t_sig[j % 2]])
                        if b == 5:
                            P.emit("dve", lambda e, j=j: e.tensor_tensor(out=sig[j % 2][:], in0=sig[j % 2][:], in1=zm[:], op=ALU.mult),
                                   reads=[t_sig[j % 2], t_zm], writes=[t_sig[j % 2]])
                        i = cnt[2] % 4
                        cnt[2] += 1
                        P.emit("dve", lambda e, j=j, i=i: e.tensor_tensor(out=stg[i][:], in0=a_st[:, j, :], in1=sig[j % 2][:], op=ALU.mult),
                               reads=[t_ast, t_sig[j % 2]], writes=[t_stg[i]])
                        P.emit("sp", lambda e, i=i, ch=ch, t0=t0: e.dma_start(out=zT_s[ch, :, t0:t0 + 512], in_=stg[i][:]),
                               reads=[t_stg[i]], writes=[tk_z], dsem="e_s%d" % i)
        P.barrier()

    def tokoff(r):
        if r < 4:
            return 2560 + r * 64
        if r < 36:
            return 512 + (r - 4) * 64
        return 2816 + (r - 36) * 64

    def even_attn():
        with ExitStack() as ph:
            def psb(name, shape, dt):
                return ph.enter_context(nc.sbuf_tensor(name + _uid(), list(shape), dt))
            kT = [psb("a_kT%d" % i, [128, NTA], BF16) for i in range(2)]
            qT = [psb("a_qT%d" % i, [128, NTO], BF16) for i in range(2)]
            V = [psb("a_V%d" % i, [128, 24, 129], BF16) for i in range(2)]
            cV = [psb("a_cV%d" % i, [128, 4, 129], BF16) for i in range(2)]
            ckf = [psb("a_ckf%d" % i, [128, 4, 128], F32) for i in range(2)]
            ckT = [psb("a_ckT%d" % i, [128, 512], BF16) for i in range(2)]
            nb = [psb("a_nb%d" % i, [128, 5 * 768], F32) for i in range(2)]
            t_h = [Tok() for _ in range(2)]
            t_ckf = [Tok() for _ in range(2)]
            t_ckT = [Tok() for _ in range(2)]
            oT = [psb("a_oT%d" % i, [128, NTO], BF16) for i in range(2)]
            t_oT = [Tok() for _ in range(2)]
            sl = [psb("a_sl%d" % i, [128, 768], F32) for i in range(2)]
            t_sl = [Tok() for _ in range(2)]
            el = [psb("a_el%d" % i, [128, 6, 128], BF16) for i in range(2)]
            t_el = [Tok() for _ in range(2)]
            ec = [psb("a_ec%d" % i, [128, 4, 128], BF16) for i in range(2)]
            t_ec = [Tok() for _ in range(2)]
            ep = [psb("a_ep%d" % i, [128, 2, 256], BF16) for i in range(2)]
            t_ep = [Tok() for _ in range(2)]
            rz = [psb("a_rz%d" % i, [128, 1], F32) for i in range(2)]
            on = [psb("a_on%d" % i, [128, 128], F32) for i in range(2)]
            t_on = [Tok() for _ in range(2)]
            for i in range(2):
                P.emit("dve", lambda e, i=i: e.memset(V[i][:, :, 128:129], 1.0), writes=[t_h[i]])
                P.emit("dve", lambda e, i=i: e.memset(cV[i][:, :, 128:129], 1.0), writes=[t_h[i]])
            cnt = [0]

            def fin1(po, k):
                P.emit("dve", lambda e, k=k, po=po: e.reciprocal(out=rz[k][:], in_=PS[po][:, 128:129]),
                       reads=[PT[po]], writes=[t_on[k]])
                P.emit("dve", lambda e, k=k, po=po: e.tensor_scalar(out=on[k][:], in0=PS[po][:, 0:128], scalar1=rz[k][:, 0:1],
                                                                    scalar2=None, op0=ALU.mult),
                       reads=[PT[po], t_on[k]], writes=[t_on[k]])

            def fin2(po, k, hb, q0):
                P.emit("pe", lambda e, k=k, po=po: e.transpose(out=PS[po][:, 256:384], in_=on[k][:], identity=ident[:]),
                       reads=[t_on[k], t_const], writes=[PT[po]])
                P.emit("act", lambda e, po=po, hb=hb, q0=q0: e.activation(out=oT[hb][:, q0:q0 + 128], in_=PS[po][:, 256:384], func=AF.Copy),
                       reads=[PT[po]], writes=[t_oT[hb]])

            def finish(po, hb, q0):
                k = cnt[0] % 2
                cnt[0] += 1
                fin1(po, k)
                fin2(po, k, hb, q0)

            def loads(h):
                hb = h % 2
                P.emit("sp", lambda e, h=h, hb=hb: e.dma_start(out=kT[hb][:], in_=kT_s[h]), reads=[tk_k], writes=[t_h[hb]], dsem="a_l%d" % hb)
                P.emit("sp", lambda e, h=h, hb=hb: e.dma_start(out=qT[hb][:], in_=qT_s[h]), reads=[tk_q], writes=[t_h[hb]], dsem="a_l%d" % hb)
                P.emit("sp", lambda e, h=h, hb=hb: e.dma_start(
                    out=V[hb][:, :, 0:128], in_=v_s[:, h * 128:(h + 1) * 128].rearrange("(t p) d -> p t d", p=128)),
                    reads=[tk_v], writes=[t_h[hb]], dsem="a_l%d" % hb)
                P.emit("sp", lambda e, h=h, hb=hb: e.dma_start(out=nb[hb][:], in_=nb_d[h]), writes=[t_h[hb]], dsem="a_l%d" % hb)
                P.emit("pool", lambda e, h=h, hb=hb: e.dma_start(
                    out=cV[hb][:, :, 0:128], in_=cv_a[h].rearrange("(t p) d -> p t d", p=128)),
                    writes=[t_h[hb]], dsem="a_c%d" % hb)
                P.emit("sp", lambda e, h=h, hb=hb: e.dma_start(out=ckf[hb][:], in_=ck_a[h].rearrange("(t p) d -> p t d", p=128)),
                       writes=[t_ckf[hb]], dsem="a_k%d" % hb)
                for t in range(4):
                    P.emit("pe", lambda e, t=t, hb=hb: e.transpose(out=PS[6][:, t * 128:(t + 1) * 128], in_=ckf[hb][:, t, :], identity=ident[:]),
                           reads=[t_ckf[hb], t_const], writes=[PT[6]])
                P.emit("dve", lambda e, hb=hb: e.tensor_copy(out=ckT[hb][:], in_=PS[6][:]), reads=[PT[6]], writes=[t_ckT[hb]])

            loads(0)
            for h in range(8):
                hb = h % 2
                if h + 1 < 8:
                    loads(h + 1)
                for s_ in range(2):
                    pb = s_ % 2
                    for kt in range(2):
                        P.emit("pe", lambda e, s_=s_, kt=kt, hb=hb, pb=pb: e.matmul(
                            PS[pb][:, kt * 256:(kt + 1) * 256], lhsT=kT[hb][:, s_ * 256 + kt * 128:s_ * 256 + (kt + 1) * 128],
                            rhs=qT[hb][:, s_ * 256:(s_ + 1) * 256], start=True, stop=True),
                            reads=[t_h[hb]], writes=[PT[pb]])
                    P.emit("act", lambda e, pb=pb: e.activation(out=ep[pb][:].rearrange("p a b -> p (a b)"), in_=PS[pb][:], func=AF.Exp, scale=SC_A),
                           reads=[PT[pb]], writes=[t_ep[pb]])
                    for qt in range(2):
                        po = 3 if qt == 0 else 7
                        for kt in range(2):
                            P.emit("pe", lambda e, qt=qt, kt=kt, pb=pb, hb=hb, s_=s_, po=po: e.matmul(
                                PS[po][:, 0:129], lhsT=ep[pb][:, kt, qt * 128:(qt + 1) * 128], rhs=V[hb][:, s_ * 2 + kt, :],
                                start=(kt == 0), stop=(kt == 1)),
                                reads=[t_ep[pb], t_h[hb]], writes=[PT[po]])
                        finish(po, hb, s_ * 256 + qt * 128)
                def banks(p):
                    return ((0, 1, 2, 3) if p % 2 == 0 else (4, 5, 6, 7))

                def qk(p):
                    ws = min(max(2 * p, 0), 28)
                    q0 = 512 + p * 128
                    bl0, bl1, bc, _ = banks(p)
                    for t in range(6):
                        ko = tokoff(ws + 2 * t)
                        bank, col = (bl0, t) if t < 4 else (bl1, t - 4)
                        P.emit("pe", lambda e, ko=ko, bank=bank, col=col, hb=hb, q0=q0: e.matmul(
                            PS[bank][:, col * 128:(col + 1) * 128], lhsT=kT[hb][:, ko:ko + 128], rhs=qT[hb][:, q0:q0 + 128],
                            start=True, stop=True),
                            reads=[t_h[hb]], writes=[PT[bank]])
                    for t in range(4):
                        P.emit("pe", lambda e, t=t, hb=hb, q0=q0, bc=bc: e.matmul(
                            PS[bc][:, t * 128:(t + 1) * 128], lhsT=ckT[hb][:, t * 128:(t + 1) * 128], rhs=qT[hb][:, q0:q0 + 128],
                            start=True, stop=True),
                            reads=[t_ckT[hb], t_h[hb]], writes=[PT[bc]])

                def mid(p):
                    var = {0: 1, 1: 2, 14: 3, 15: 4}.get(p, 0)
                    k = p % 2
                    bl0, bl1, bc, _ = banks(p)
                    P.emit("dve", lambda e, k=k, hb=hb, var=var, bl0=bl0: e.scalar_tensor_tensor(
                        out=sl[k][:, 0:512], in0=PS[bl0][:], scalar=SC_A, in1=nb[hb][:, var * 768:var * 768 + 512],
                        op0=ALU.mult, op1=ALU.add), reads=[PT[bl0], t_h[hb]], writes=[t_sl[k]])
                    P.emit("dve", lambda e, k=k, hb=hb, var=var, bl1=bl1: e.scalar_tensor_tensor(
                        out=sl[k][:, 512:768], in0=PS[bl1][:, 0:256], scalar=SC_A, in1=nb[hb][:, var * 768 + 512:var * 768 + 768],
                        op0=ALU.mult, op1=ALU.add), reads=[PT[bl1], t_h[hb]], writes=[t_sl[k]])
                    P.emit("act", lambda e, k=k, bc=bc: e.activation(out=ec[k][:].rearrange("p a b -> p (a b)"), in_=PS[bc][:], func=AF.Exp, scale=SC_A),
                           reads=[PT[bc]], writes=[t_ec[k]])
                    P.emit("act", lambda e, k=k: e.activation(out=el[k][:].rearrange("p a b -> p (a b)"), in_=sl[k][:], func=AF.Exp),
                           reads=[t_sl[k]], writes=[t_el[k]])

                def pv(p):
                    ws = min(max(2 * p, 0), 28)
                    k = p % 2
                    po = banks(p)[3]
                    for t in range(4):
                        P.emit("pe", lambda e, t=t, k=k, hb=hb, po=po: e.matmul(
                            PS[po][:, 0:129], lhsT=ec[k][:, t, :], rhs=cV[hb][:, t, :], start=(t == 0), stop=False),
                            reads=[t_ec[k], t_h[hb]], writes=[PT[po]])
                    for t in range(6):
                        vt = tokoff(ws + 2 * t) // 128
                        P.emit("pe", lambda e, t=t, k=k, hb=hb, po=po, vt=vt: e.matmul(
                            PS[po][:, 0:129], lhsT=el[k][:, t, :], rhs=V[hb][:, vt, :], start=False, stop=(t == 5)),
                            reads=[t_el[k], t_h[hb]], writes=[PT[po]])

                qk(0)
                for p in range(16):
                    mid(p)
                    if p + 1 < 16:
                        qk(p + 1)
                    pv(p)
                    fin1(banks(p)[3], p % 2)
                    if p >= 1:
                        fin2(banks(p - 1)[3], (p - 1) % 2, hb, 512 + (p - 1) * 128)
                fin2(banks(15)[3], 1, hb, 512 + 15 * 128)
                P.emit("sp", lambda e, h=h, hb=hb: e.dma_start(out=mixT_s[h], in_=oT[hb][:]),
                       reads=[t_oT[hb]], writes=[tk_mix], dsem="a_o%d" % hb)
        P.barrier()

    def even_conv():
        with ExitStack() as ph:
            def psb(name, shape, dt):
                return ph.enter_context(nc.sbuf_tensor(name + _uid(), list(shape), dt))
            LS = 2560 + 30
            zp = psb("c_zp", [128, 8, LS], BF16)
            zq = psb("c_zq", [128, 8, 2, 286], BF16)
            t_z = Tok()
            dg = psb("c_dg", [128, 8 * 31, 128], BF16)
            dwT = psb("c_dwT", [128, 8 * 31], F32)
            cvp = psb("c_cvp", [128, 24], F32)
            idb = psb("c_idb", [128, 128], BF16)
            onesf = psb("c_onesf", [128, 128], F32)
            t_c = Tok()
            cvb = psb("c_cv", [128, 8, 512], F32)
            t_cv = [Tok() for _ in range(8)]
            sqf = [psb("c_sqf%d" % i, [128, 512], F32) for i in range(2)]
            t_sqf = [Tok() for _ in range(2)]
            mean = psb("c_mean", [128, 512], F32)
            rstd = psb("c_rstd", [128, 512], F32)
            t_st = Tok()
            tmp = [psb("c_tmp%d" % i, [128, 512], F32) for i in range(2)]
            t_tmp = [Tok() for _ in range(2)]
            stg = [psb("c_stg%d" % i, [128, 512], BF16) for i in range(2)]
            t_stg = [Tok() for _ in range(2)]
            P.emit("sp", lambda e: e.dma_start(out=dwT[:], in_=dwT_d), writes=[t_c], dsem="c_c")
            P.emit("sp", lambda e: e.dma_start(out=cvp[:], in_=cvp_d), writes=[t_c], dsem="c_c")
            P.emit("dve", lambda e: e.tensor_copy(out=idb[:], in_=ident[:]), reads=[t_const], writes=[t_c])
            P.emit("dve", lambda e: e.memset(onesf[:], 1.0), writes=[t_c])
            P.emit("dve", lambda e: e.memset(zp[:, :, 0:15], 0.0), writes=[t_z])
            P.emit("dve", lambda e: e.memset(zp[:, :, LS - 15:LS], 0.0), writes=[t_z])
            P.emit("dve", lambda e: e.memset(zq[:].rearrange("p a b c -> p (a b c)"), 0.0), writes=[t_z])
            for ch in range(8):
                P.emit("sp", lambda e, ch=ch: e.dma_start(out=zp[:, ch, 15:15 + 256], in_=zT_s[ch, :, 2560:2816]),
                       reads=[tk_z], writes=[t_z], dsem="c_z")
                P.emit("sp", lambda e, ch=ch: e.dma_start(out=zp[:, ch, 15 + 256:15 + 2304], in_=zT_s[ch, :, 512:2560]),
                       reads=[tk_z], writes=[t_z], dsem="c_z")
                P.emit("sp", lambda e, ch=ch: e.dma_start(out=zp[:, ch, 15 + 2304:15 + 2560], in_=zT_s[ch, :, 2816:3072]),
                       reads=[tk_z], writes=[t_z], dsem="c_z")
                for s_ in range(2):
                    P.emit("sp", lambda e, ch=ch, s_=s_: e.dma_start(out=zq[:, ch, s_, 15:15 + 256], in_=zT_s[ch, :, s_ * 256:(s_ + 1) * 256]),
                           reads=[tk_z], writes=[t_z], dsem="c_z")
                for j in range(31):
                    P.emit("dve", lambda e, ch=ch, j=j: e.tensor_scalar(
                        out=dg[:, ch * 31 + j, :], in0=idb[:], scalar1=dwT[:, ch * 31 + j:ch * 31 + j + 1], scalar2=None, op0=ALU.mult),
                        reads=[t_c], writes=[t_c])
            for pc in range(5):
                for ch in range(8):
                    py = ch % 4
                    if pc == 0:
                        for s_ in range(2):
                            for j in range(31):
                                P.emit("pe", lambda e, ch=ch, j=j, s_=s_, py=py: e.matmul(
                                    PS[py][:, s_ * 256:(s_ + 1) * 256], lhsT=dg[:, ch * 31 + j, :], rhs=zq[:, ch, s_, j:j + 256],
                                    start=(j == 0 and s_ == 0), stop=(j == 30 and s_ == 1)),
                                    reads=[t_c, t_z], writes=[PT[py]])
                    else:
                        o0 = 256 + (pc - 1) * 512
                        for j in range(31):
                            P.emit("pe", lambda e, ch=ch, j=j, o0=o0, py=py: e.matmul(
                                PS[py][:], lhsT=dg[:, ch * 31 + j, :], rhs=zp[:, ch, o0 + j:o0 + j + 512],
                                start=(j == 0), stop=(j == 30)),
                                reads=[t_c, t_z], writes=[PT[py]])
                    P.emit("act", lambda e, ch=ch, py=py: e.activation(out=cvb[:, ch, :], in_=PS[py][:], func=AF.Identity,
                                                                       bias=cvp[:, ch:ch + 1], scale=1.0),
                           reads=[PT[py], t_c], writes=[t_cv[ch]])
                    P.emit("pe", lambda e, ch=ch: e.matmul(PS[4][:], lhsT=onesf[:], rhs=cvb[:, ch, :], start=(ch == 0), stop=(ch == 7)),
                           reads=[t_cv[ch], t_c], writes=[PT[4]])
                    P.emit("act", lambda e, ch=ch: e.activation(out=sqf[ch % 2][:], in_=cvb[:, ch, :], func=AF.Square),
                           reads=[t_cv[ch]], writes=[t_sqf[ch % 2]])
                    P.emit("pe", lambda e, ch=ch: e.matmul(PS[5][:], lhsT=onesf[:], rhs=sqf[ch % 2][:], start=(ch == 0), stop=(ch == 7)),
                           reads=[t_sqf[ch % 2], t_c], writes=[PT[5]])
                P.emit("dve", lambda e: e.tensor_scalar(out=mean[:], in0=PS[4][:], scalar1=1.0 / 1024, scalar2=None, op0=ALU.mult),
                       reads=[PT[4]], writes=[t_st])
                P.emit("dve", lambda e: e.tensor_tensor(out=rstd[:], in0=mean[:], in1=mean[:], op=ALU.mult), reads=[t_st], writes=[t_st])
                P.emit("dve", lambda e: e.scalar_tensor_tensor(out=rstd[:], in0=PS[5][:], scalar=1.0 / 1024, in1=rstd[:],
                                                               op0=ALU.mult, op1=ALU.subtract), reads=[PT[5], t_st], writes=[t_st])
                P.emit("dve", lambda e: e.tensor_scalar(out=rstd[:], in0=rstd[:], scalar1=EPS, scalar2=None, op0=ALU.add),
                       reads=[t_st], writes=[t_st])
                P.emit("act", lambda e: e.activation(out=rstd[:], in_=rstd[:], func=AF.Sqrt), reads=[t_st], writes=[t_st])
                P.emit("dve", lambda e: e.reciprocal(out=rstd[:], in_=rstd[:]), reads=[t_st], writes=[t_st])
                t0 = pc * 512
                for ch in range(8):
                    k = ch % 2
                    P.emit("dve", lambda e, ch=ch, k=k: e.tensor_tensor(out=tmp[k][:], in0=cvb[:, ch, :], in1=mean[:], op=ALU.subtract),
                           reads=[t_cv[ch], t_st], writes=[t_tmp[k]])
                    P.emit("dve", lambda e, k=k: e.tensor_tensor(out=tmp[k][:], in0=tmp[k][:], in1=rstd[:], op=ALU.mult),
                           reads=[t_tmp[k], t_st], writes=[t_tmp[k]])
                    P.emit("act", lambda e, ch=ch, k=k: e.activation(out=stg[k][:], in_=tmp[k][:], func=AF.Silu,
                                                                     bias=cvp[:, 16 + ch:17 + ch], scale=cvp[:, 8 + ch:9 + ch]),
                           reads=[t_tmp[k], t_c], writes=[t_stg[k]])
                    P.emit("sp", lambda e, ch=ch, k=k, t0=t0: e.dma_start(out=mixT_s[8 + ch, :, t0:t0 + 512], in_=stg[k][:]),
                           reads=[t_stg[k]], writes=[tk_mix], dsem="c_s%d" % k)
        P.barrier()

    c_w_in = din("c_w_in", [D, 6144])
    c_w_out = din("c_w_out", [D, D])
    ck_c = din("cache_c_k", [16, 512, 128])
    cv_c = din("cache_c_v", [8, 512, 256])
    lamb_d = din("lamb", [1, 512])
    sg_d = din("subg", [1, 256])
    cos_d = din("ropec", [128, 2048])
    sin_d = din("ropes", [128, 2048])
    perm_d = din("perm_in", [128, 128])
    nck_d = dout("nck", [512, 2048])
    ncv_d = dout("ncv", [512, 2048])
    q1T_s = dscr("q1T_s", [16, 128, NTO], BF16, out=dev)
    kpT_s = dscr("kpT_s", [16, 128, 512], BF16)
    vp_s = dscr("vp_s", [512, 2048], BF16)
    kx = [dscr("kx%d" % i, [256, 2048], BF16) for i in range(8)]
    vx = [dscr("vx%d" % i, [256, 2048], BF16) for i in range(8)]
    kxa = [dscr("kxa%d" % i, [512, 2048], BF16) for i in range(8)]
    vxa = [dscr("vxa%d" % i, [512, 2048], BF16) for i in range(8)]
    tk_q1, tk_kp, tk_vp, tk_kx, tk_vx, tk_kxa, tk_vxa = [Tok() for _ in range(7)]
    LAM_INIT = 0.8 - 0.6 * math.exp(-0.3 * 1)
    neglam = sb("neglam", [128, 1], F32)
    sgt = sb("sgt", [128, 256], F32)
    t_lam = Tok()

    def odd_setup():
        with ExitStack() as ph:
            def psb(name, shape, dt):
                return ph.enter_context(nc.sbuf_tensor(name + _uid(), list(shape), dt))
            lb = psb("l_lb", [128, 512], F32)
            pr = psb("l_pr", [128, 256], F32)
            e2 = psb("l_e2", [128, 2], F32)
            P.emit("sp", lambda e: e.dma_start(out=lb[:], in_=lamb_d.to_broadcast([128, 512])), writes=[t_lam], dsem="l_l")
            P.emit("sp", lambda e: e.dma_start(out=sgt[:], in_=sg_d.to_broadcast([128, 256])), writes=[t_lam], dsem="l_l")
            P.emit("dve", lambda e: e.tensor_tensor(out=pr[:, 0:128], in0=lb[:, 0:128], in1=lb[:, 128:256], op=ALU.mult), reads=[t_lam], writes=[t_lam])
            P.emit("dve", lambda e: e.tensor_tensor(out=pr[:, 128:256], in0=lb[:, 256:384], in1=lb[:, 384:512], op=ALU.mult), reads=[t_lam], writes=[t_lam])
            P.emit("dve", lambda e: e.tensor_reduce(out=e2[:], in_=pr[:].rearrange("p (a b) -> p a b", a=2), axis=mybir.AxisListType.X, op=ALU.add),
                   reads=[t_lam], writes=[t_lam])
            P.emit("act", lambda e: e.activation(out=e2[:], in_=e2[:], func=AF.Exp), reads=[t_lam], writes=[t_lam])
            P.emit("dve", lambda e: e.tensor_tensor(out=neglam[:], in0=e2[:, 1:2], in1=e2[:, 0:1], op=ALU.subtract), reads=[t_lam], writes=[t_lam])
            P.emit("dve", lambda e: e.tensor_scalar(out=neglam[:], in0=neglam[:], scalar1=-LAM_INIT, scalar2=None, op0=ALU.add), reads=[t_lam], writes=[t_lam])
            P.emit("dve", lambda e: e.tensor_scalar(out=sgt[:], in0=sgt[:], scalar1=1.0 - LAM_INIT, scalar2=None, op0=ALU.mult), reads=[t_lam], writes=[t_lam])
        P.barrier()

    def odd_proj():
        wr = c_w_in.rearrange("(k p) n -> p k n", p=128)
        with ExitStack() as ph:
            def psb(name, shape, dt):
                return ph.enter_context(nc.sbuf_tensor(name + _uid(), list(shape), dt))
            xT = psb("p_xT", [128, 16, 512], F32)
            t_xT = [Tok() for _ in range(16)]
            hT2 = [psb("p_hT", [128, 16, 512], BF16) for _ in range(2)]
            t_hT2 = [[Tok() for _ in range(16)] for _ in range(2)]
            bufs = norm_bufs(psb, "p_")
            wsl = [psb("p_w%d" % i, [128, 16, 512], BF16) for i in range(3)]
            t_w = [Tok() for _ in range(3)]
            stg = [psb("p_stg%d" % i, [128, 512], BF16) for i in range(4)]
            t_stg = [Tok() for _ in range(4)]
            stf = [psb("p_stf%d" % i, [128, 512], F32) for i in range(2)]
            t_stf = [Tok() for _ in range(2)]
            qf = [psb("p_qf%d" % i, [128, 512], F32) for i in range(2)]
            t_qf = [Tok() for _ in range(2)]
            r1 = [psb("p_r1%d" % i, [128, 512], F32) for i in range(2)]
            t_r1 = [Tok() for _ in range(2)]
            cosT = psb("p_cos", [128, 2048], F32)
            sinT = psb("p_sin", [128, 2048], F32)
            perm = psb("p_perm", [128, 128], F32)
            t_rp = Tok()
            P.emit("sp", lambda e: e.dma_start(out=cosT[:], in_=cos_d), writes=[t_rp], dsem="p_c")
            P.emit("sp", lambda e: e.dma_start(out=sinT[:], in_=sin_d), writes=[t_rp], dsem="p_c")
            P.emit("sp", lambda e: e.dma_start(out=perm[:], in_=perm_d), writes=[t_rp], dsem="p_c")
            cnt = [0, 0, 0, 0, 0]

            def loadw(g):
                s = cnt[0] % 3
                cnt[0] += 1
                P.emit("pool", lambda e, s=s, g=g: e.dma_start(out=wsl[s][:], in_=wr[:, :, g * 512:(g + 1) * 512]),
                       writes=[t_w[s]], dsem="p_w%d" % s)
                return s

            norm_block(psb, 0, 1, 1, xT, t_xT, hT2[0], t_hT2[0], bufs)
            for b in range(5):
                hT, t_hT = hT2[b % 2], t_hT2[b % 2]
                t0 = b * 512
                for g in range(12):
                    if g == 6 and b + 1 < 5:
                        norm_block(psb, b + 1, 1, 1, xT, t_xT, hT2[(b + 1) % 2], t_hT2[(b + 1) % 2], bufs)
                    s = loadw(g)
                    if g < 8:
                        for j in range(4):
                            mp = (g % 4) * 4 + j
                            py = cnt[1] % 4
                            cnt[1] += 1
                            for kc in range(16):
                                P.emit("pe", lambda e, s=s, j=j, kc=kc, py=py, hT=hT: e.matmul(
                                    PS[py][:], lhsT=wsl[s][:, kc, j * 128:(j + 1) * 128], rhs=hT[:, kc, :],
                                    start=(kc == 0), stop=(kc == 15)),
                                    reads=[t_w[s], t_hT[kc]], writes=[PT[py]])
                            i = cnt[2] % 4
                            cnt[2] += 1
                            if b == 0:
                                P.emit("act", lambda e, i=i, py=py: e.activation(out=stg[i][:], in_=PS[py][:], func=AF.Copy),
                                       reads=[PT[py]], writes=[t_stg[i]])
                            else:
                                k = cnt[4] % 2
                                cnt[4] += 1
                                o0 = (b - 1) * 512
                                P.emit("act", lambda e, k=k, py=py: e.activation(out=qf[k][:], in_=PS[py][:], func=AF.Copy),
                                       reads=[PT[py]], writes=[t_qf[k]])
                                P.emit("pe", lambda e, k=k: e.matmul(PS[4 + k][:], lhsT=perm[:], rhs=qf[k][:], start=True, stop=True),
                                       reads=[t_qf[k], t_rp], writes=[PT[4 + k]])
                                P.emit("dve", lambda e, k=k, o0=o0: e.tensor_tensor(out=r1[k][:], in0=qf[k][:], in1=cosT[:, o0:o0 + 512], op=ALU.mult),
                                       reads=[t_qf[k], t_rp], writes=[t_r1[k]])
                                P.emit("dve", lambda e, k=k, o0=o0: e.tensor_tensor(out=qf[k][:], in0=PS[4 + k][:], in1=sinT[:, o0:o0 + 512], op=ALU.mult),
                                       reads=[PT[4 + k], t_rp, t_r1[k]], writes=[t_qf[k]])
                                P.emit("dve", lambda e, k=k, i=i: e.tensor_tensor(out=stg[i][:], in0=qf[k][:], in1=r1[k][:], op=ALU.add),
                                       reads=[t_qf[k], t_r1[k]], writes=[t_stg[i]])
                            if g < 4:
                                P.emit("sp", lambda e, i=i, mp=mp, t0=t0: e.dma_start(out=q1T_s[mp, :, t0:t0 + 512], in_=stg[i][:]),
                                       reads=[t_stg[i]], writes=[tk_q1], dsem="p_s%d" % i)
                            elif b == 0:
                                P.emit("sp", lambda e, i=i, mp=mp: e.dma_start(out=kpT_s[mp], in_=stg[i][:]),
                                       reads=[t_stg[i]], writes=[tk_kp], dsem="p_s%d" % i)
                            else:
                                o0 = (b - 1) * 512
                                P.emit("sp", lambda e, i=i, mp=mp, o0=o0: e.dma_start(out=kx[mp // 2][(mp % 2) * 128:(mp % 2 + 1) * 128, o0:o0 + 512], in_=stg[i][:]),
                                       reads=[t_stg[i]], writes=[tk_kx], dsem="p_s%d" % i)
                    if (g >= 4 and g < 8 and b == 0) or g >= 8:
                        for ti in range(4):
                            py = cnt[1] % 4
                            cnt[1] += 1
                            for kc in range(16):
                                P.emit("pe", lambda e, s=s, ti=ti, kc=kc, py=py, hT=hT: e.matmul(
                                    PS[py][:], lhsT=hT[:, kc, ti * 128:(ti + 1) * 128], rhs=wsl[s][:, kc, :],
                                    start=(kc == 0), stop=(kc == 15)),
                                    reads=[t_w[s], t_hT[kc]], writes=[PT[py]])
                            c0 = (g % 4) * 512
                            if b == 0:
                                f = cnt[3] % 2
                                cnt[3] += 1
                                r0 = ti * 128
                                P.emit("dve", lambda e, f=f, py=py: e.tensor_copy(out=stf[f][:], in_=PS[py][:]),
                                       reads=[PT[py]], writes=[t_stf[f]])
                                od = nck_d if g < 8 else ncv_d
                                P.emit("sp", lambda e, f=f, od=od, r0=r0, c0=c0: e.dma_start(out=od[r0:r0 + 128, c0:c0 + 512], in_=stf[f][:]),
                                       reads=[t_stf[f]], dsem="p_f%d" % f)
                                if g >= 8:
                                    i = cnt[2] % 4
                                    cnt[2] += 1
                                    P.emit("act", lambda e, i=i, f=f: e.activation(out=stg[i][:], in_=stf[f][:], func=AF.Copy),
                                           reads=[t_stf[f]], writes=[t_stg[i]])
                                    P.emit("sp", lambda e, i=i, r0=r0, c0=c0: e.dma_start(out=vp_s[r0:r0 + 128, c0:c0 + 512], in_=stg[i][:]),
                                           reads=[t_stg[i]], writes=[tk_vp], dsem="p_s%d" % i)
                            else:
                                i = cnt[2] % 4
                                cnt[2] += 1
                                r0 = (b - 1) * 512 + ti * 128
                                if ti % 2:
                                    P.emit("act", lambda e, i=i, py=py: e.activation(out=stg[i][:], in_=PS[py][:], func=AF.Copy),
                                           reads=[PT[py]], writes=[t_stg[i]])
                                else:
                                    P.emit("dve", lambda e, i=i, py=py: e.tensor_copy(out=stg[i][:], in_=PS[py][:]),
                                           reads=[PT[py]], writes=[t_stg[i]])
                                P.emit("sp", lambda e, i=i, r0=r0, c0=c0: e.dma_start(out=vx[r0 // 256][r0 % 256:r0 % 256 + 128, c0:c0 + 512], in_=stg[i][:]),
                                       reads=[t_stg[i]], writes=[tk_vx], dsem="p_s%d" % i)
        P.barrier()

    def odd_cc():
        groups = [[2 * i, 2 * i + 1] for i in range(ncores // 2)]
        for i in range(8):
            P.emit("pool", lambda e, i=i: e.collective_compute("AllGather", ALU.bypass, replica_groups=groups, ins=[kx[i]], outs=[kxa[i]]),
                   reads=[tk_kx], writes=[tk_kxa], dsem="cc_k", cc=True)
            P.emit("pool", lambda e, i=i: e.collective_compute("AllGather", ALU.bypass, replica_groups=groups, ins=[vx[i]], outs=[vxa[i]]),
                   reads=[tk_vx], writes=[tk_vxa], dsem="cc_v", cc=True)
        P.barrier()

    SC_C = 128 ** -0.5

    def odd_attn():
        with ExitStack() as ph:
            def psb(name, shape, dt):
                return ph.enter_context(nc.sbuf_tensor(name + _uid(), list(shape), dt))
            kT1_ = [psb("d_kT1", [128, 2, 4096], BF16) for _ in range(2)]
            q1T_ = [psb("d_q1T", [128, 2, NTO], BF16) for _ in range(2)]
            kpT_ = [psb("d_kpT", [128, 2, 512], BF16) for _ in range(2)]
            ckf_ = [psb("d_ckf", [128, 2, 4, 128], F32) for _ in range(2)]
            ckT_ = [psb("d_ckT", [128, 2, 512], BF16) for _ in range(2)]
            V1_ = [psb("d_V1", [128, 32, 257], BF16) for _ in range(2)]
            cV_ = [psb("d_cV", [128, 4, 257], BF16) for _ in range(2)]
            Vp_ = [psb("d_Vp", [128, 4, 257], BF16) for _ in range(2)]
            t_h_ = [Tok() for _ in range(2)]
            t_ckf_ = [Tok() for _ in range(2)]
            t_ckT_ = [Tok() for _ in range(2)]
            cur = {}
            eb = [psb("d_e%d" % i, [128, 512], BF16) for i in range(4)]
            t_e = [Tok() for _ in range(4)]
            On = [psb("d_On%d" % i, [128, 2, 4, 256], F32) for i in range(2)]
            t_On = [[[Tok() for _ in range(4)] for _ in range(2)] for _ in range(2)]
            rz = [psb("d_rz%d" % i, [128, 1], F32) for i in range(4)]
            t_rz = [Tok() for _ in range(4)]
            ob = [psb("d_ob%d" % i, [128, 4, 256], F32) for i in range(2)]
            t_ob = [[Tok() for _ in range(4)] for _ in range(2)]
            junk = psb("d_junk", [128, 256], F32)
            t_junk = Tok()
            ssq = [psb("d_ssq%d" % i, [128, 4], F32) for i in range(2)]
            t_ssq = [Tok() for _ in range(2)]
            oT1 = psb("d_oT", [128, 2, NTO], BF16)
            t_oT = Tok()
            for i in range(2):
                P.emit("dve", lambda e, i=i: e.memset(V1_[i][:, :, 256:257], 1.0), writes=[t_h_[i]])
                P.emit("dve", lambda e, i=i: e.memset(cV_[i][:, :, 256:257], 1.0), writes=[t_h_[i]])
                P.emit("dve", lambda e, i=i: e.memset(Vp_[i][:, :, 256:257], 1.0), writes=[t_h_[i]])
            cnt = [0, 0, 0]
            pending = [None]
            SB = [0, 1, 6]

            def flush():
                if pending[0] is not None:
                    f = pending[0]
                    pending[0] = None
                    f()

            def block(qap, ktiles, nq, q0):
                nqt = nq // 128
                nk = len(ktiles)
                bp = cnt[2] % 2
                cnt[2] += 1
                for m in range(2):
                    def S(kt, m=m):
                        pb = SB[kt % 3]
                        P.emit("pe", lambda e, kt=kt, pb=pb, m=m: e.matmul(
                            PS[pb][:, 0:nq], lhsT=ktiles[kt][0](m), rhs=qap(m), start=True, stop=True),
                            reads=[cur["t_h"], cur["t_ckT"]], writes=[PT[pb]])
                    S(0)
                    if nk > 1:
                        S(1)
                    for kt in range(nk):
                        if kt + 2 < nk:
                            S(kt + 2)
                        ei = cnt[0] % 4
                        cnt[0] += 1
                        pb = SB[kt % 3]
                        P.emit("act", lambda e, ei=ei, pb=pb: e.activation(out=eb[ei][:, 0:nq], in_=PS[pb][:, 0:nq], func=AF.Exp, scale=SC_C),
                               reads=[PT[pb]], writes=[t_e[ei]])
                        for qt in range(nqt):
                            P.emit("pe", lambda e, ei=ei, qt=qt, kt=kt: e.matmul(
                                PS[2 + qt][:, 0:257], lhsT=eb[ei][:, qt * 128:(qt + 1) * 128], rhs=ktiles[kt][1],
                                start=(kt == 0), stop=(kt == nk - 1)),
                                reads=[t_e[ei], cur["t_h"]], writes=[PT[2 + qt]])
                    for qt in range(nqt):
                        k = cnt[1] % 4
                        cnt[1] += 1
                        po = 2 + qt
                        P.emit("dve", lambda e, k=k, po=po: e.reciprocal(out=rz[k][:], in_=PS[po][:, 256:257]),
                               reads=[PT[po]], writes=[t_rz[k]])
                        P.emit("dve", lambda e, k=k, po=po, qt=qt, m=m: e.tensor_scalar(
                            out=On[bp][:, m, qt, :], in0=PS[po][:, 0:256], scalar1=rz[k][:, 0:1], scalar2=None, op0=ALU.mult),
                            reads=[PT[po], t_rz[k]], writes=[t_On[bp][m][qt]])
                    if m == 0:
                        flush()
                for qt in range(nqt):
                    P.emit("dve", lambda e, qt=qt: e.scalar_tensor_tensor(
                        out=ob[bp][:, qt, :], in0=On[bp][:, 1, qt, :], scalar=neglam[:, 0:1], in1=On[bp][:, 0, qt, :],
                        op0=ALU.mult, op1=ALU.add),
                        reads=[t_On[bp][0][qt], t_On[bp][1][qt], t_lam], writes=[t_ob[bp][qt]])
                    P.emit("dve", lambda e, qt=qt: e.scalar_tensor_tensor(
                        out=junk[:], in0=ob[bp][:, qt, :], scalar=1.0, in1=ob[bp][:, qt, :], op0=ALU.mult, op1=ALU.mult,
                        accum_out=ssq[bp][:, qt:qt + 1]),
                        reads=[t_ob[bp][qt]], writes=[t_junk, t_ssq[bp]])

                def tail(bp=bp, nqt=nqt, q0=q0):
                    P.emit("dve", lambda e: e.tensor_scalar(out=ssq[bp][:, 0:nqt], in0=ssq[bp][:, 0:nqt], scalar1=1.0 / 256, scalar2=EPS,
                                                            op0=ALU.mult, op1=ALU.add), reads=[t_ssq[bp]], writes=[t_ssq[bp]])
                    P.emit("act", lambda e: e.activation(out=ssq[bp][:, 0:nqt], in_=ssq[bp][:, 0:nqt], func=AF.Sqrt),
                           reads=[t_ssq[bp]], writes=[t_ssq[bp]])
                    P.emit("dve", lambda e: e.reciprocal(out=ssq[bp][:, 0:nqt], in_=ssq[bp][:, 0:nqt]), reads=[t_ssq[bp]], writes=[t_ssq[bp]])
                    for qt in range(nqt):
                        P.emit("dve", lambda e, qt=qt: e.scalar_tensor_tensor(
                            out=ob[bp][:, qt, :], in0=ob[bp][:, qt, :], scalar=ssq[bp][:, qt:qt + 1], in1=sgt[:], op0=ALU.mult, op1=ALU.mult),
                            reads=[t_ob[bp][qt], t_ssq[bp], t_lam], writes=[t_ob[bp][qt]])
                        for c in range(2):
                            P.emit("pe", lambda e, qt=qt, c=c: e.transpose(out=PS[7][:, c * 128:(c + 1) * 128],
                                                                           in_=ob[bp][:, qt, c * 128:(c + 1) * 128], identity=ident[:]),
                                   reads=[t_ob[bp][qt], t_const], writes=[PT[7]])
                        qq = q0 + qt * 128
                        P.emit("dve", lambda e, qq=qq: e.tensor_copy(out=oT1[:, :, qq:qq + 128],
                                                                     in_=PS[7][:, 0:256].rearrange("p (c t) -> p c t", c=2)),
                               reads=[PT[7]], writes=[t_oT])
                pending[0] = tail

            def loads(hd):
                hb = hd % 2
                kT1, q1T, kpT, ckf, V1, cV, Vp = kT1_[hb], q1T_[hb], kpT_[hb], ckf_[hb], V1_[hb], cV_[hb], Vp_[hb]
                t_h, t_ckf = t_h_[hb], t_ckf_[hb]
                for m in range(2):
                    mp = 2 * hd + m
                    for r in range(2):
                        P.emit("sp", lambda e, m=m, mp=mp, r=r: e.dma_start(
                            out=kT1[:, m, r * 2048:(r + 1) * 2048],
                            in_=kxa[mp // 2][r * 256 + (mp % 2) * 128:r * 256 + (mp % 2 + 1) * 128, :]),
                            reads=[tk_kxa], writes=[t_h], dsem="d_l%d" % hb)
                    P.emit("sp", lambda e, m=m, mp=mp: e.dma_start(out=q1T[:, m, :], in_=q1T_s[mp]), reads=[tk_q1], writes=[t_h], dsem="d_l%d" % hb)
                    P.emit("sp", lambda e, m=m, mp=mp: e.dma_start(out=kpT[:, m, :], in_=kpT_s[mp]), reads=[tk_kp], writes=[t_h], dsem="d_l%d" % hb)
                    P.emit("sp", lambda e, m=m, mp=mp: e.dma_start(out=ckf[:, m], in_=ck_c[mp].rearrange("(t p) d -> p t d", p=128)),
                           writes=[t_ckf], dsem="d_k%d" % hb)
                for r in range(2):
                    for i in range(8):
                        P.emit("sp", lambda e, hd=hd, r=r, i=i: e.dma_start(
                            out=V1[:, r * 16 + 2 * i:r * 16 + 2 * i + 2, 0:256],
                            in_=vxa[i][r * 256:(r + 1) * 256, hd * 256:(hd + 1) * 256].rearrange("(t p) d -> p t d", p=128)),
                            reads=[tk_vxa], writes=[t_h], dsem="d_l%d" % hb)
                P.emit("sp", lambda e, hd=hd: e.dma_start(
                    out=Vp[:, :, 0:256], in_=vp_s[:, hd * 256:(hd + 1) * 256].rearrange("(t p) d -> p t d", p=128)),
                    reads=[tk_vp], writes=[t_h], dsem="d_l%d" % hb)
                P.emit("pool", lambda e, hd=hd: e.dma_start(
                    out=cV[:, :, 0:256], in_=cv_c[hd].rearrange("(t p) d -> p t d", p=128)),
                    writes=[t_h], dsem="d_c%d" % hb)

            def cktr(hd):
                hb = hd % 2
                for m in range(2):
                    for t in range(4):
                        P.emit("pe", lambda e, t=t, m=m: e.transpose(out=PS[7][:, t * 128:(t + 1) * 128], in_=ckf_[hb][:, m, t, :], identity=ident[:]),
                               reads=[t_ckf_[hb], t_const], writes=[PT[7]])
                    P.emit("dve", lambda e, m=m: e.tensor_copy(out=ckT_[hb][:, m, :], in_=PS[7][:]), reads=[PT[7]], writes=[t_ckT_[hb]])

            loads(0)
            cktr(0)
            for hd in range(8):
                hb = hd % 2
                kT1, q1T, kpT, ckT, V1, cV, Vp = kT1_[hb], q1T_[hb], kpT_[hb], ckT_[hb], V1_[hb], cV_[hb], Vp_[hb]
                cur["t_h"], cur["t_ckT"] = t_h_[hb], t_ckT_[hb]
                if hd + 1 < 8:
                    loads(hd + 1)
                for s_ in range(2):
                    kts = [((lambda m, s_=s_, kt=kt, kpT=kpT: kpT[:, m, s_ * 256 + kt * 128:s_ * 256 + (kt + 1) * 128]), Vp[:, s_ * 2 + kt, :])
                           for kt in range(2)]
                    block(lambda m, s_=s_, q1T=q1T: q1T[:, m, s_ * 256:(s_ + 1) * 256], kts, 256, s_ * 256)
                kts = [((lambda m, kt=kt, ckT=ckT: ckT[:, m, kt * 128:(kt + 1) * 128]), cV[:, kt, :]) for kt in range(4)]
                kts += [((lambda m, kt=kt, kT1=kT1: kT1[:, m, kt * 128:(kt + 1) * 128]), V1[:, kt, :]) for kt in range(32)]
                for qb in range(4):
                    block(lambda m, qb=qb, q1T=q1T: q1T[:, m, 512 + qb * 512:512 + (qb + 1) * 512], kts, 512, 512 + qb * 512)
                    if qb == 2 and hd + 1 < 8:
                        cktr(hd + 1)
                flush()
                for c in range(2):
                    P.emit("sp", lambda e, hd=hd, c=c: e.dma_start(out=mixT_s[2 * hd + c], in_=oT1[:, c, :]),
                           reads=[t_oT], writes=[tk_mix], dsem="d_o")
        P.barrier()

    import os
    ph_env = os.environ.get("PHASES")
    phs = set(ph_env.split(",")) if ph_env else None

    def on(name):
        return phs is None or name in phs
    if on("f00"):
        ffn_phase(0, 0, 6)
    if on("even"):
        even_proj()
        even_attn()
        even_conv()
        wout_phase(0, a_w_out, mixT_s, tk_mix)
    if on("f01"):
        ffn_phase(0, 1, 5)
    if on("f10"):
        ffn_phase(1, 0, 5)
    if on("odd_setup"):
        odd_setup()
    if on("odd_proj"):
        odd_proj()
    if on("odd_cc"):
        odd_cc()
    if on("odd_attn"):
        odd_attn()
    if on("odd_wout"):
        wout_phase(1, c_w_out, mixT_s, tk_mix)
    if on("f11"):
        ffn_phase(1, 1, 5)
    final_phase()
    P.emit("sp", None)
    P.build()
    top.close()
    return nc, P


_CACHE = {}


def _build_nbias(rpb, half):
    base = 0 if half == 0 else 32
    out = np.full((8, 128, 5, 6, 128), -30000.0, np.float32)
    kc = np.arange(64)
    qc = np.arange(64)
    cs = np.clip(qc - 8, 0, 48)
    colmask = (kc[:, None] >= cs[None, :]) & (kc[:, None] < cs[None, :] + 16)
    cidx = np.clip(kc[:, None] - qc[None, :] + 15, 0, 30)
    for var, p in {0: 7, 1: 0, 2: 1, 3: 14, 4: 15}.items():
        ws = min(max(2 * p, 0), 28)
        for qr2 in range(2):
            r = 2 * p + qr2 + base
            rs = min(max(r - 4, 0), 56)
            for t in range(6):
                for kr2 in range(2):
                    kr = ws + 2 * t + kr2 - 4 + base
                    if rs <= kr < rs + 8:
                        vals = rpb[:, kr - r + 7][:, cidx]
                        out[:, kr2 * 64:(kr2 + 1) * 64, var, t, qr2 * 64:(qr2 + 1) * 64] = np.where(colmask[None], vals, -30000.0)
    return out.reshape(8, 128, 5 * 768)


def _rope_tables(own0):
    t = own0 + np.arange(2048)
    inv = (np.float32(10000.0) ** (-np.arange(32, dtype=np.float32) / np.float32(32))).astype(np.float32)
    row = (t // 64).astype(np.float32)
    col = (t % 64).astype(np.float32)
    cosT = np.zeros((128, 2048), np.float32)
    sinT = np.zeros((128, 2048), np.float32)
    for d in range(128):
        ang = ((row if d < 64 else col) * inv[d % 32]).astype(np.float32)
        cosT[d] = np.cos(ang)
        sn = np.sin(ang)
        sinT[d] = -sn if (d % 64) < 32 else sn
    return cosT, sinT


def _perm():
    p = np.zeros((128, 128), np.float32)
    for m in range(128):
        p[m + 32 if (m % 64) < 32 else m - 32, m] = 1.0
    return p


def _core_inputs(c, I):
    b, half = c // 2, c % 2
    own0 = half * 2048
    xs = I["x_sample"][b]
    xin = np.zeros((NTA, D), np.float32)
    xin[0:512] = I["x_prompt"][2 * c:2 * c + 2].reshape(512, D)
    xin[512:2560] = xs[own0:own0 + 2048]
    if half == 1:
        xin[2560:2816] = xs[own0 - 256:own0]
    else:
        xin[2816:3072] = xs[own0 + 2048:own0 + 2304]
    cT = np.zeros((128, 16, 2), np.float32)
    cT[:, :, 0] = I["c_ctx"].reshape(16, 128).T
    cT[:, :, 1] = I["c"][b].reshape(16, 128).T
    ngT = I["norm_g"].reshape(6, 16, 128).transpose(2, 0, 1).reshape(128, 96)
    zmask = np.zeros((1, 512), np.float32)
    if half == 0:
        zmask[0, 256:] = 1.0
    else:
        zmask[0, :256] = 1.0
    convp = np.concatenate([I["b_dw_b"][0].reshape(8, 128).T, I["b_ln_g"][0].reshape(8, 128).T,
                            I["b_ln_b"][0].reshape(8, 128).T], axis=1)
    dwT = I["b_dw_w"][0].reshape(31, 8, 128).transpose(2, 1, 0).reshape(128, 248)
    wmh = np.ascontiguousarray(I["w_mod"].reshape(2, D, 9, 2, 1024)[:, :, :, half, :]).reshape(2, D, 9 * 1024)
    bmh = np.ascontiguousarray(I["b_mod"].reshape(2, 9, 2, 1024)[:, :, half, :]).reshape(2, 9 * 1024)
    m = dict(xin=xin, cT=cT.reshape(128, 32), ngT=np.ascontiguousarray(ngT), w_mod_h=wmh, b_mod_h=bmh,
             nbias=_build_nbias(I["a_rpb"][0], half), zmask=zmask, convp=np.ascontiguousarray(convp),
             dwT=np.ascontiguousarray(dwT), cache_a_k=np.ascontiguousarray(I["cache_a_k"][b, 0]),
             cache_a_v=np.ascontiguousarray(I["cache_a_v"][b, 0]),
             cache_c_k=np.ascontiguousarray(I["cache_c_k"][b, 0]), cache_c_v=np.ascontiguousarray(I["cache_c_v"][b, 0]),
             lamb=np.ascontiguousarray(I["c_lambda"][0].reshape(1, 512)), subg=np.ascontiguousarray(I["c_subln_g"][0].reshape(1, 256)),
             ropec=_rope_tables(own0)[0], ropes=_rope_tables(own0)[1], perm_in=_perm(),
             fgT=np.ascontiguousarray(I["final_g"].reshape(16, 128).T),
             ident_in=np.eye(128, dtype=np.float32))
    return m


def kernel(**inputs):
    I = {k: np.asarray(v) for k, v in inputs.items()}
    if "nc" not in _CACHE:
        _CACHE["nc"] = build_program()[0]
    nc = _CACHE["nc"]
    shared = dict(ffn_w1=I["ffn_w1"], ffn_w3=I["ffn_w3"], ffn_w2=I["ffn_w2"],
                  a_w_in=I["a_w_in"][0], a_w_out=I["a_w_out"][0], c_w_in=I["c_w_in"][0], c_w_out=I["c_w_out"][0])
    in_maps = []
    for c in range(8):
        m = _core_inputs(c, I)
        m.update(shared)
        in_maps.append(m)
    res = run_bass_kernel_spmd(nc, in_maps, core_ids=list(range(8)))
    R = res.results
    y_prompt = np.concatenate([R[c]["y"][0:512].reshape(2, 256, D) for c in range(8)], axis=0)
    y_sample = np.stack([np.concatenate([R[2 * b]["y"][512:2560], R[2 * b + 1]["y"][512:2560]], axis=0)
                         for b in range(4)], axis=0)
    def heads(name, nh, dh):
        return np.concatenate([R[c][name].reshape(2, 256, nh, dh).transpose(0, 2, 1, 3) for c in range(8)], axis=0)[:, None]
    return (y_prompt, y_sample, heads("nak", 8, 128), heads("nav", 8, 128), heads("nck", 16, 128), heads("ncv", 8, 256))
```

```python
import math
from contextlib import ExitStack
import numpy as np
import concourse.bass as bass
import concourse.mybir as mybir
from concourse.bass_utils import run_bass_kernel_spmd

F32 = mybir.dt.float32
BF16 = mybir.dt.bfloat16
AF = mybir.ActivationFunctionType
ALU = mybir.AluOpType
ENGS = ("pe", "act", "dve", "pool", "sp")

D = 2048
DFF = 5632
NKC = 16
NFC = 44
EPS = 1e-6
NTA = 3072
NTO = 2560


class Tok:
    __slots__ = ("lw", "rd")

    def __init__(self):
        self.lw = None
        self.rd = {}


class Op:
    __slots__ = ("eng", "fn", "deps", "signal", "is_dma", "dsem", "val", "cc")

    def __init__(self, eng, fn, is_dma, dsem):
        self.cc = False
        self.eng = eng
        self.fn = fn
        self.deps = []
        self.signal = is_dma
        self.is_dma = is_dma
        self.dsem = dsem
        self.val = None


class Prog:
    def __init__(self, nc):
        self.nc = nc
        self.ops = {e: [] for e in ENGS}
        self.last_dma = {}
        self.bar = None
        self.bar_seen = set()
        self.n = 0

    def barrier(self):
        deps = []
        for e in ENGS:
            for op in reversed(self.ops[e]):
                if not op.is_dma and op.fn is not None:
                    deps.append(op)
                    break
        deps.extend(self.last_dma.values())
        for d in deps:
            d.signal = True
        self.bar = deps
        self.bar_seen = set()

    def emit(self, eng, fn, reads=(), writes=(), dsem=None, cc=False):
        is_dma = dsem is not None
        op = Op(eng, fn, is_dma, dsem)
        op.cc = cc
        deps = {}
        for t in reads:
            if t.lw is not None:
                deps[id(t.lw)] = t.lw
        for t in writes:
            if t.lw is not None:
                deps[id(t.lw)] = t.lw
            for r in t.rd.values():
                deps[id(r)] = r
        if is_dma:
            p = self.last_dma.get(dsem)
            if p is not None:
                deps[id(p)] = p
            self.last_dma[dsem] = op
        if self.bar is not None and eng not in self.bar_seen:
            self.bar_seen.add(eng)
            for d in self.bar:
                deps[id(d)] = d
        for d in deps.values():
            if d is op:
                continue
            if (not is_dma) and eng == "pe" and d.eng == "pe" and not d.is_dma:
                continue
            d.signal = True
            op.deps.append(d)
        key = ("d", dsem) if is_dma else eng
        for t in reads:
            t.rd[key] = op
        for t in writes:
            t.lw = op
            t.rd = {}
        self.ops[eng].append(op)
        self.n += 1
        return op

    def build(self):
        nc = self.nc
        es = ExitStack()
        esem = {e: es.enter_context(nc.semaphore("s_" + e)) for e in ENGS}
        dsems = {}
        for e in ENGS:
            for op in self.ops[e]:
                if op.is_dma and op.dsem not in dsems:
                    dsems[op.dsem] = es.enter_context(nc.semaphore("d_%s" % (op.dsem,)))
        cnt = {e: 0 for e in ENGS}
        dcnt = {k: 0 for k in dsems}
        for e in ENGS:
            for op in self.ops[e]:
                if op.is_dma:
                    dcnt[op.dsem] += (1 if op.cc else 16)
                    op.val = (dsems[op.dsem], dcnt[op.dsem], ("d", op.dsem))
                elif op.signal and op.fn is not None:
                    cnt[e] += 1
                    op.val = (esem[e], cnt[e], e)
        self.stats = dict(cnt=cnt, n=self.n, nd=len(dsems), dmax=max(dcnt.values()) if dcnt else 0)
        handles = dict(pe="tensor", act="scalar", dve="vector", pool="gpsimd", sp="sync")
        with nc.Block() as block:
            for e in ENGS:
                ops = self.ops[e]

                def body(eng, ops=ops):
                    known = {}
                    for op in ops:
                        for d in op.deps:
                            if d.val is None:
                                continue
                            sem, val, key = d.val
                            if known.get(key, 0) >= val:
                                continue
                            known[key] = val
                            eng.wait_ge(sem, val)
                        if op.fn is None:
                            continue
                        ins = op.fn(eng)
                        if op.val is not None:
                            if op.cc:
                                ins.then_inc(op.val[0])
                            else:
                                ins.then_inc(op.val[0], 16 if op.is_dma else 1)

                getattr(block, handles[e])(body)
        es.close()


_UID = [0]


def _uid():
    _UID[0] += 1
    return "_u%d" % _UID[0]


def build_program(stop=99, dev=False, ncores=8):
    nc = bass.Bass("TRN2", target_bir_lowering=False)
    P = Prog(nc)

    def din(name, shape, dt=F32):
        return nc.dram_tensor(name, list(shape), dt, kind="ExternalInput").ap()

    def dout(name, shape, dt=F32):
        return nc.dram_tensor(name, list(shape), dt, kind="ExternalOutput").ap()

    def dscr(name, shape, dt, out=False):
        return nc.dram_tensor(name, list(shape), dt, kind="ExternalOutput" if out else "Internal").ap()

    xin = din("xin", [NTA, D])
    cT_d = din("cT", [128, 32])
    w_mod = din("w_mod_h", [2, D, 9 * 1024])
    b_mod = din("b_mod_h", [2, 9 * 1024])
    ag_in = dscr("ag_in", [36, 1024], F32)
    ag_out = dscr("ag_out", [72, 1024], F32)
    ngT_d = din("ngT", [128, 6 * 16])
    w1_d = din("ffn_w1", [2, 2, D, DFF])
    w3_d = din("ffn_w3", [2, 2, D, DFF])
    w2_d = din("ffn_w2", [2, 2, DFF, D])
    fgT_d = din("fgT", [128, 16])
    y_d = dout("y", [NTO, D])

    xT_s = dscr("xT_s", [128, NKC, NTA], F32, out=dev)
    tk_xTs = [[Tok() for _ in range(16)] for _ in range(6)]

    top = ExitStack()

    def sb(name, shape, dt):
        return top.enter_context(nc.sbuf_tensor(name, list(shape), dt))

    ident = sb("ident", [128, 128], F32)
    ones_bf = sb("ones_bf", [128, 128], BF16)
    modA = sb("modA", [128, 2, 3, 16, 2], F32)
    modB = sb("modB", [128, 2, 3, 16, 2], F32)
    modG = sb("modG", [128, 2, 3, 16, 2], F32)
    fgT = sb("fgT_sb", [128, 16], F32)
    t_const = Tok()
    t_mod = Tok()
    PS = [top.enter_context(nc.psum_tensor("ps%d" % i, [128, 512], F32)) for i in range(8)]
    PT = [Tok() for _ in range(8)]

    ident_d = din("ident_in", [128, 128])
    P.emit("sp", lambda e: e.dma_start(out=ident[:], in_=ident_d), writes=[t_const], dsem="c0")
    P.emit("dve", lambda e: e.memset(ones_bf[:], 1.0), writes=[t_const])
    P.emit("sp", lambda e: e.dma_start(out=fgT[:], in_=fgT_d), writes=[t_const], dsem="c0")

    ph_s0 = ExitStack()
    if True:
        def psb(name, shape, dt, ph=ph_s0):
            return ph.enter_context(nc.sbuf_tensor(name + _uid(), list(shape), dt))
        cT = psb("cT_sb", [128, 32], F32)
        sT = psb("sT_sb", [128, 32], BF16)
        ngT = psb("ngT_sb", [128, 96], F32)
        wm = [psb("wm%d" % i, [128, 2048], BF16) for i in range(4)]
        t_wm = [Tok() for _ in range(4)]
        bm = [psb("bm%d" % i, [2, 2048], F32) for i in range(2)]
        t_bm = [Tok() for _ in range(2)]
        mrow = [psb("mrow%d" % i, [2, 2048], F32) for i in range(2)]
        t_mrow = [Tok() for _ in range(2)]
        modT = psb("modT", [128, 2, 9, 16, 2], F32)
        t_c = Tok()
        P.emit("sp", lambda e: e.dma_start(out=cT[:], in_=cT_d), writes=[t_c], dsem="c1")
        P.emit("sp", lambda e: e.dma_start(out=ngT[:], in_=ngT_d), writes=[t_c], dsem="c1")
        P.emit("act", lambda e: e.activation(out=sT[:], in_=cT[:], func=AF.Silu), reads=[t_c], writes=[t_c])
        sTv = sT[:].rearrange("p (c n) -> p c n", n=2)
        t_agi, t_ago = Tok(), Tok()

        def s0_units():
            it = 0
            for l in range(2):
                for j0 in range(0, 9, 2):
                    js = [j for j in (j0, j0 + 1) if j < 9]
                    width = 1024 * len(js)
                    for j in js:
                        b = j % 2
                        P.emit("sp", lambda e, l=l, j=j, b=b: e.dma_start(
                            out=bm[b][:, 0:1024], in_=b_mod[l:l + 1, j * 1024:(j + 1) * 1024].to_broadcast([2, 1024])),
                            writes=[t_bm[b]], dsem="bm%d" % b)
                    for kc in range(16):
                        s = it % 4
                        it += 1
                        P.emit("pool", lambda e, l=l, j0=j0, kc=kc, s=s, width=width: e.dma_start(
                            out=wm[s][:, 0:width], in_=w_mod[l, kc * 128:(kc + 1) * 128, j0 * 1024:j0 * 1024 + width]),
                            writes=[t_wm[s]], dsem="wm%d" % s)
                        for nb in range(width // 512):
                            P.emit("pe", lambda e, kc=kc, s=s, nb=nb: e.matmul(
                                PS[nb][0:2, :], lhsT=sTv[:, kc, :], rhs=wm[s][:, nb * 512:(nb + 1) * 512],
                                start=(kc == 0), stop=(kc == 15)),
                                reads=[t_c, t_wm[s]], writes=[PT[nb]])
                        yield
                    for ji, j in enumerate(js):
                        b = j % 2
                        for nb in range(2):
                            P.emit("dve", lambda e, b=b, nb=nb, ji=ji: e.tensor_tensor(
                                out=mrow[b][:, nb * 512:(nb + 1) * 512], in0=PS[2 * ji + nb][0:2, :],
                                in1=bm[b][:, nb * 512:(nb + 1) * 512], op=ALU.add),
                                reads=[PT[2 * ji + nb], t_bm[b]], writes=[t_mrow[b]])
                        r0 = (l * 9 + j) * 2
                        P.emit("sp", lambda e, b=b, r0=r0: e.dma_start(out=ag_in[r0:r0 + 2, :], in_=mrow[b][:, 0:1024]),
                               reads=[t_mrow[b]], writes=[t_agi], dsem="ag_s")

        s0_gen = s0_units()

    ph_s1 = ExitStack()
    if True:
        ph = ph_s1

        def psb(name, shape, dt):
            return ph.enter_context(nc.sbuf_tensor(name + _uid(), list(shape), dt))
        xt = [psb("xt%d" % i, [128, D], F32) for i in range(2)]
        t_xt = [Tok() for _ in range(2)]
        st = [psb("st%d" % i, [128, 16, 512], F32) for i in range(2)]
        t_st = [Tok() for _ in range(2)]
        for blk in range(6):
            sb_ = blk % 2
            for ti in range(4):
                t = blk * 4 + ti
                xb = t % 2
                P.emit("sp", lambda e, t=t, xb=xb: e.dma_start(out=xt[xb][:], in_=xin[t * 128:(t + 1) * 128, :]),
                       writes=[t_xt[xb]], dsem="xt%d" % xb)
                for g in range(4):
                    for c4 in range(4):
                        c = g * 4 + c4
                        P.emit("pe", lambda e, xb=xb, c=c, g=g, c4=c4: e.transpose(
                            out=PS[4 + g][:, c4 * 128:(c4 + 1) * 128], in_=xt[xb][:, c * 128:(c + 1) * 128],
                            identity=ident[:]),
                            reads=[t_xt[xb], t_const], writes=[PT[4 + g]])
                    eng = "dve" if g % 2 == 0 else "act"
                    if eng == "dve":
                        P.emit("dve", lambda e, g=g, sb_=sb_, ti=ti: e.tensor_copy(
                            out=st[sb_][:, g * 4:(g + 1) * 4, ti * 128:(ti + 1) * 128],
                            in_=PS[4 + g][:].rearrange("p (c t) -> p c t", c=4)),
                            reads=[PT[4 + g]], writes=[t_st[sb_]])
                    else:
                        P.emit("act", lambda e, g=g, sb_=sb_, ti=ti: e.activation(
                            out=st[sb_][:, g * 4:(g + 1) * 4, ti * 128:(ti + 1) * 128],
                            in_=PS[4 + g][:].rearrange("p (c t) -> p c t", c=4), func=AF.Copy),
                            reads=[PT[4 + g]], writes=[t_st[sb_]])
            for _ in range(28):
                next(s0_gen, None)
            P.emit("sp", lambda e, blk=blk, sb_=sb_: e.dma_start(
                out=xT_s[:, :, blk * 512:(blk + 1) * 512], in_=st[sb_][:]),
                reads=[t_st[sb_]], writes=tk_xTs[blk], dsem="st%d" % sb_)
        for _ in s0_gen:
            pass

        P.emit("pool", lambda e: e.collective_compute("AllGather", ALU.bypass,
                                                      replica_groups=[[2 * i, 2 * i + 1] for i in range(ncores // 2)],
                                                      ins=[ag_in], outs=[ag_out]),
               reads=[t_agi], writes=[t_ago], dsem="cc_m", cc=True)
        for l in range(2):
            for j in range(9):
                b = (l * 9 + j) % 2
                r0 = (l * 9 + j) * 2
                for r in range(2):
                    P.emit("sp", lambda e, b=b, r0=r0, r=r: e.dma_start(
                        out=mrow[b][:, r * 1024:(r + 1) * 1024], in_=ag_out[r * 36 + r0:r * 36 + r0 + 2, :]),
                        reads=[t_ago], writes=[t_mrow[b]], dsem="ag_l%d" % b)
                for c in range(16):
                    P.emit("pe", lambda e, b=b, c=c: e.matmul(
                        PS[4][:, c * 2:c * 2 + 2], lhsT=mrow[b][0:2, c * 128:(c + 1) * 128], rhs=ident[0:2, 0:2],
                        start=True, stop=True),
                        reads=[t_mrow[b], t_const], writes=[PT[4]])
                P.emit("dve", lambda e, l=l, j=j: e.tensor_copy(
                    out=modT[:, l, j].rearrange("p c n -> p (c n)"), in_=PS[4][:, 0:32]),
                    reads=[PT[4]], writes=[t_mod])
        for l in range(2):
            for i in range(3):
                for cnd in range(2):
                    P.emit("dve", lambda e, l=l, i=i, cnd=cnd: e.scalar_tensor_tensor(
                        out=modA[:, l, i, :, cnd], in0=modT[:, l, 3 * i + 1, :, cnd], scalar=1.0,
                        in1=ngT[:, (l * 3 + i) * 16:(l * 3 + i + 1) * 16], op0=ALU.add, op1=ALU.mult),
                        reads=[t_mod, t_c], writes=[t_mod])
                P.emit("dve", lambda e, l=l, i=i: e.tensor_copy(
                    out=modB[:, l, i].rearrange("p c n -> p (c n)"),
                    in_=modT[:, l, 3 * i].rearrange("p c n -> p (c n)")), reads=[t_mod], writes=[t_mod])
                P.emit("dve", lambda e, l=l, i=i: e.tensor_scalar(
                    out=modG[:, l, i].rearrange("p c n -> p (c n)"),
                    in0=modT[:, l, 3 * i + 2].rearrange("p c n -> p (c n)"),
                    scalar1=(1.0 if i == 1 else 0.5), scalar2=None, op0=ALU.mult),
                    reads=[t_mod], writes=[t_mod])
    P.barrier()
    ph_s1.close()
    ph_s0.close()

    def ffn_phase(l, half, nblocks):
        sub = 0 if half == 0 else 2
        w1 = w1_d[l, half]
        w3 = w3_d[l, half]
        w2 = w2_d[l, half].rearrange("(k p) n -> p k n", p=128)
        w1r = w1.rearrange("(k p) n -> p k n", p=128)
        w3r = w3.rearrange("(k p) n -> p k n", p=128)
        with ExitStack() as ph:
            def psb(name, shape, dt):
                return ph.enter_context(nc.sbuf_tensor(name + _uid(), list(shape), dt))
            xT = psb("f_xT", [128, 16, 512], F32)
            t_xT = [Tok() for _ in range(16)]
            hT2 = [psb("f_hT", [128, 16, 512], BF16) for _ in range(2)]
            t_hT2 = [[Tok() for _ in range(16)] for _ in range(2)]
            gT = psb("f_gT", [128, NFC, 512], BF16)
            t_gT = [Tok() for _ in range(NFC)]
            arena = psb("f_w", [128, 4 * 8192], BF16)
            t_w = [Tok() for _ in range(4)]
            lru = [0, 0, 0, 0]
            clock = [0]
            bufs = norm_bufs(psb, "f_")
            sil = [psb("f_sil%d" % i, [128, 512], F32) for i in range(2)]
            t_sil = [Tok() for _ in range(2)]
            xc = [psb("f_xc%d" % i, [128, 512], F32) for i in range(3)]
            t_xc = [Tok() for _ in range(3)]

            def wslot1():
                r = min(range(4), key=lambda i: lru[i])
                clock[0] += 1
                lru[r] = clock[0]
                return r

            def wslot2():
                pr = 0 if max(lru[0], lru[1]) <= max(lru[2], lru[3]) else 1
                clock[0] += 1
                lru[2 * pr] = lru[2 * pr + 1] = clock[0]
                return pr

            norm_block(psb, 0, l, sub, xT, t_xT, hT2[0], t_hT2[0], bufs)
            xi = 0
            for b in range(nblocks):
                cnd = 0 if b == 0 else 1
                G = modG[:, l, sub, :, cnd]
                hT, t_hT = hT2[b % 2], t_hT2[b % 2]
                for fg in range(NFC // 4):
                    s1 = wslot1()
                    s3 = wslot1()
                    v1 = arena[:, s1 * 8192:(s1 + 1) * 8192].rearrange("p (k n) -> p k n", k=16)
                    v3 = arena[:, s3 * 8192:(s3 + 1) * 8192].rearrange("p (k n) -> p k n", k=16)
                    P.emit("pool", lambda e, v1=v1, fg=fg: e.dma_start(out=v1, in_=w1r[:, :, fg * 512:(fg + 1) * 512]),
                           writes=[t_w[s1]], dsem="w%d" % s1)
                    P.emit("pool", lambda e, v3=v3, fg=fg: e.dma_start(out=v3, in_=w3r[:, :, fg * 512:(fg + 1) * 512]),
                           writes=[t_w[s3]], dsem="w%d" % s3)
                    for j in range(4):
                        fc = fg * 4 + j
                        p1 = fc % 2
                        p3 = 2 + fc % 2
                        for kc in range(16):
                            P.emit("pe", lambda e, v1=v1, j=j, kc=kc, p1=p1, hT=hT: e.matmul(
                                PS[p1][:], lhsT=v1[:, kc, j * 128:(j + 1) * 128], rhs=hT[:, kc, :],
                                start=(kc == 0), stop=(kc == 15)),
                                reads=[t_w[s1], t_hT[kc]], writes=[PT[p1]])
                        for kc in range(16):
                            P.emit("pe", lambda e, v3=v3, j=j, kc=kc, p3=p3, hT=hT: e.matmul(
                                PS[p3][:], lhsT=v3[:, kc, j * 128:(j + 1) * 128], rhs=hT[:, kc, :],
                                start=(kc == 0), stop=(kc == 15)),
                                reads=[t_w[s3], t_hT[kc]], writes=[PT[p3]])
                        P.emit("act", lambda e, fc=fc, p1=p1: e.activation(out=sil[fc % 2][:], in_=PS[p1][:], func=AF.Silu),
                               reads=[PT[p1]], writes=[t_sil[fc % 2]])
                        P.emit("dve", lambda e, fc=fc, p3=p3: e.tensor_tensor(
                            out=gT[:, fc, :], in0=sil[fc % 2][:], in1=PS[p3][:], op=ALU.mult),
                            reads=[t_sil[fc % 2], PT[p3]], writes=[t_gT[fc]])
                    if fg == 3 and b + 1 < nblocks:
                        norm_block(psb, b + 1, l, sub, xT, t_xT, hT2[(b + 1) % 2], t_hT2[(b + 1) % 2], bufs)
                for ng in range(8):
                    pr = wslot2()
                    v2 = arena[:, pr * 16384:pr * 16384 + NFC * 256].rearrange("p (k n) -> p k n", k=NFC)
                    P.emit("pool", lambda e, v2=v2, ng=ng: e.dma_start(out=v2, in_=w2[:, :, ng * 256:(ng + 1) * 256]),
                           writes=[t_w[2 * pr], t_w[2 * pr + 1]], dsem="w%d" % (2 * pr))
                    for jj in range(2):
                        n_ = ng * 2 + jj
                        py = 4 + n_ % 2
                        k = xi % 3
                        xi += 1
                        P.emit("sp", lambda e, k=k, n_=n_, b=b: e.dma_start(out=xc[k][:], in_=xT_s[:, n_, b * 512:(b + 1) * 512]),
                               reads=[tk_xTs[b][n_]], writes=[t_xc[k]], dsem="f_c%d" % k)
                        for kc in range(NFC):
                            P.emit("pe", lambda e, v2=v2, jj=jj, kc=kc, py=py: e.matmul(
                                PS[py][:], lhsT=v2[:, kc, jj * 128:(jj + 1) * 128], rhs=gT[:, kc, :],
                                start=(kc == 0), stop=(kc == NFC - 1)),
                                reads=[t_w[2 * pr], t_w[2 * pr + 1], t_gT[kc]], writes=[PT[py]])
                        P.emit("dve", lambda e, n_=n_, py=py, G=G, k=k: e.scalar_tensor_tensor(
                            out=xc[k][:], in0=PS[py][:], scalar=G[:, n_:n_ + 1], in1=xc[k][:],
                            op0=ALU.mult, op1=ALU.add),
                            reads=[PT[py], t_xc[k], t_mod], writes=[t_xc[k]])
                        P.emit("sp", lambda e, k=k, n_=n_, b=b: e.dma_start(out=xT_s[:, n_, b * 512:(b + 1) * 512], in_=xc[k][:]),
                               reads=[t_xc[k]], writes=[tk_xTs[b][n_]], dsem="f_c%d" % k)
        P.barrier()

    def final_phase():
        with ExitStack() as ph:
            def psb(name, shape, dt):
                return ph.enter_context(nc.sbuf_tensor(name + _uid(), list(shape), dt))
            xT = psb("z_xT", [128, 16, 512], F32)
            t_xT = Tok()
            sq = [psb("z_sq%d" % i, [128, 512], BF16) for i in range(2)]
            t_sq = [Tok() for _ in range(2)]
            rstd = psb("z_rstd", [128, 512], F32)
            t_rstd = Tok()
            yt = [psb("z_yt%d" % i, [128, D], F32) for i in range(2)]
            t_yt = [Tok() for _ in range(2)]
            for b in range(5):
                P.emit("sp", lambda e, b=b: e.dma_start(out=xT[:], in_=xT_s[:, :, b * 512:(b + 1) * 512]),
                       reads=tk_xTs[b], writes=[t_xT], dsem="z_x")
                for c in range(16):
                    P.emit("act", lambda e, c=c: e.activation(out=sq[c % 2][:], in_=xT[:, c, :], func=AF.Square),
                           reads=[t_xT], writes=[t_sq[c % 2]])
                    P.emit("pe", lambda e, c=c: e.matmul(PS[6][:], lhsT=ones_bf[:], rhs=sq[c % 2][:],
                                                         start=(c == 0), stop=(c == 15)),
                           reads=[t_sq[c % 2], t_const], writes=[PT[6]])
                P.emit("dve", lambda e: e.tensor_scalar(out=rstd[:], in0=PS[6][:], scalar1=1.0 / D, scalar2=EPS,
                                                        op0=ALU.mult, op1=ALU.add), reads=[PT[6]], writes=[t_rstd])
                P.emit("act", lambda e: e.activation(out=rstd[:], in_=rstd[:], func=AF.Sqrt),
                       reads=[t_rstd], writes=[t_rstd])
                P.emit("dve", lambda e: e.reciprocal(out=rstd[:], in_=rstd[:]), reads=[t_rstd], writes=[t_rstd])
                for c in range(16):
                    P.emit("dve", lambda e, c=c: e.scalar_tensor_tensor(
                        out=xT[:, c, :], in0=xT[:, c, :], scalar=fgT[:, c:c + 1], in1=rstd[:],
                        op0=ALU.mult, op1=ALU.mult),
                        reads=[t_xT, t_rstd, t_const], writes=[t_xT])
                for ti in range(4):
                    yb = ti % 2
                    for g in range(4):
                        for c4 in range(4):
                            c = g * 4 + c4
                            P.emit("pe", lambda e, c=c, g=g, c4=c4, ti=ti: e.transpose(
                                out=PS[g][:, c4 * 128:(c4 + 1) * 128], in_=xT[:, c, ti * 128:(ti + 1) * 128],
                                identity=ident[:]),
                                reads=[t_xT, t_const], writes=[PT[g]])
                        if g % 2 == 0:
                            P.emit("dve", lambda e, g=g, yb=yb: e.tensor_copy(
                                out=yt[yb][:, g * 512:(g + 1) * 512], in_=PS[g][:]),
                                reads=[PT[g]], writes=[t_yt[yb]])
                        else:
                            P.emit("act", lambda e, g=g, yb=yb: e.activation(
                                out=yt[yb][:, g * 512:(g + 1) * 512], in_=PS[g][:], func=AF.Copy),
                                reads=[PT[g]], writes=[t_yt[yb]])
                    t0 = b * 512 + ti * 128
                    P.emit("sp", lambda e, t0=t0, yb=yb: e.dma_start(out=y_d[t0:t0 + 128, :], in_=yt[yb][:]),
                           reads=[t_yt[yb]], dsem="z_y%d" % yb)
        P.barrier()

    def norm_block(psb, b, l, sub, xT, t_xT, hT, t_hT, bufs):
        cnd = 0 if b == 0 else 1
        A = modA[:, l, sub, :, cnd]
        B = modB[:, l, sub, :, cnd]
        sq, t_sq, rstd, t_rstd, tmp, t_tmp = bufs
        P.emit("sp", lambda e, b=b: e.dma_start(out=xT[:], in_=xT_s[:, :, b * 512:(b + 1) * 512]),
               reads=tk_xTs[b], writes=t_xT, dsem="n_x")
        for c in range(16):
            P.emit("act", lambda e, c=c: e.activation(out=sq[c % 2][:], in_=xT[:, c, :], func=AF.Square),
                   reads=[t_xT[c]], writes=[t_sq[c % 2]])
            P.emit("pe", lambda e, c=c: e.matmul(PS[6][:], lhsT=ones_bf[:], rhs=sq[c % 2][:],
                                                 start=(c == 0), stop=(c == 15)),
                   reads=[t_sq[c % 2], t_const], writes=[PT[6]])
        P.emit("dve", lambda e: e.tensor_scalar(out=rstd[:], in0=PS[6][:], scalar1=1.0 / D, scalar2=EPS,
                                                op0=ALU.mult, op1=ALU.add), reads=[PT[6]], writes=[t_rstd])
        P.emit("act", lambda e: e.activation(out=rstd[:], in_=rstd[:], func=AF.Sqrt),
               reads=[t_rstd], writes=[t_rstd])
        P.emit("dve", lambda e: e.reciprocal(out=rstd[:], in_=rstd[:]), reads=[t_rstd], writes=[t_rstd])
        for c in range(16):
            P.emit("dve", lambda e, c=c: e.scalar_tensor_tensor(
                out=tmp[c % 2][:], in0=xT[:, c, :], scalar=A[:, c:c + 1], in1=rstd[:],
                op0=ALU.mult, op1=ALU.mult),
                reads=[t_xT[c], t_rstd, t_mod], writes=[t_tmp[c % 2]])
            P.emit("act", lambda e, c=c: e.activation(
                out=hT[:, c, :], in_=tmp[c % 2][:], func=AF.Identity, bias=B[:, c:c + 1], scale=1.0),
                reads=[t_tmp[c % 2], t_mod], writes=[t_hT[c]])

    def norm_bufs(psb, pfx):
        sq = [psb(pfx + "sq%d" % i, [128, 512], BF16) for i in range(2)]
        rstd = psb(pfx + "rstd", [128, 512], F32)
        tmp = [psb(pfx + "tmp%d" % i, [128, 512], F32) for i in range(2)]
        return (sq, [Tok(), Tok()], rstd, Tok(), tmp, [Tok(), Tok()])

    def wout_phase(l, w_out_d, mixT_s, tk_mix):
        wr = w_out_d.rearrange("(k p) n -> p k n", p=128)
        with ExitStack() as ph:
            def psb(name, shape, dt):
                return ph.enter_context(nc.sbuf_tensor(name + _uid(), list(shape), dt))
            wo = psb("o_w", [128, 16, D], BF16)
            t_wo = Tok()
            xT = psb("o_xT", [128, 16, 512], F32)
            t_xT = [Tok() for _ in range(16)]
            mx = [psb("o_mx%d" % i, [128, 16, 512], BF16) for i in range(2)]
            t_mx = [Tok() for _ in range(2)]
            for g in range(4):
                P.emit("pool", lambda e, g=g: e.dma_start(out=wo[:, :, g * 512:(g + 1) * 512],
                                                          in_=wr[:, :, g * 512:(g + 1) * 512]),
                       writes=[t_wo], dsem="o_w")
            for b in range(5):
                cnd = 0 if b == 0 else 1
                G = modG[:, l, 1, :, cnd]
                m = b % 2
                P.emit("sp", lambda e, b=b, m=m: e.dma_start(
                    out=mx[m][:], in_=mixT_s[:, :, b * 512:(b + 1) * 512].rearrange("c p t -> p c t")),
                    reads=[tk_mix], writes=[t_mx[m]], dsem="o_m%d" % m)
                P.emit("sp", lambda e, b=b: e.dma_start(out=xT[:], in_=xT_s[:, :, b * 512:(b + 1) * 512]),
                       reads=tk_xTs[b], writes=t_xT, dsem="o_x")
                for n_ in range(16):
                    py = n_ % 4
                    for kc in range(16):
                        P.emit("pe", lambda e, n_=n_, kc=kc, py=py, m=m: e.matmul(
                            PS[py][:], lhsT=wo[:, kc, n_ * 128:(n_ + 1) * 128], rhs=mx[m][:, kc, :],
                            start=(kc == 0), stop=(kc == 15)),
                            reads=[t_wo, t_mx[m]], writes=[PT[py]])
                    P.emit("dve", lambda e, n_=n_, py=py, G=G: e.scalar_tensor_tensor(
                        out=xT[:, n_, :], in0=PS[py][:], scalar=G[:, n_:n_ + 1], in1=xT[:, n_, :],
                        op0=ALU.mult, op1=ALU.add),
                        reads=[PT[py], t_xT[n_], t_mod], writes=[t_xT[n_]])
                P.emit("sp", lambda e, b=b: e.dma_start(out=xT_s[:, :, b * 512:(b + 1) * 512], in_=xT[:]),
                       reads=t_xT, writes=tk_xTs[b], dsem="o_xs")
        P.barrier()

    a_w_in = din("a_w_in", [D, 5120])
    a_w_out = din("a_w_out", [D, D])
    nb_d = din("nbias", [8, 128, 5 * 768])
    ck_a = din("cache_a_k", [8, 512, 128])
    cv_a = din("cache_a_v", [8, 512, 128])
    dwT_d = din("dwT", [128, 8 * 31])
    cvp_d = din("convp", [128, 24])
    zmask_d = din("zmask", [1, 512])
    nak_d = dout("nak", [512, 1024])
    nav_d = dout("nav", [512, 1024])
    qT_s = dscr("qT_s", [8, 128, NTO], BF16)
    kT_s = dscr("kT_s", [8, 128, NTA], BF16)
    v_s = dscr("v_s", [NTA, 1024], BF16)
    zT_s = dscr("zT_s", [8, 128, NTA], BF16)
    mixT_s = dscr("mixT_s", [16, 128, NTO], BF16, out=dev)
    tk_q, tk_k, tk_v, tk_z, tk_mix = Tok(), Tok(), Tok(), Tok(), Tok()
    SC_A = 128 ** -0.5

    def even_proj():
        wr = a_w_in.rearrange("(k p) n -> p k n", p=128)
        with ExitStack() as ph:
            def psb(name, shape, dt):
                return ph.enter_context(nc.sbuf_tensor(name + _uid(), list(shape), dt))
            xT = psb("e_xT", [128, 16, 512], F32)
            t_xT = [Tok() for _ in range(16)]
            hT2 = [psb("e_hT", [128, 16, 512], BF16) for _ in range(2)]
            t_hT2 = [[Tok() for _ in range(16)] for _ in range(2)]
            ch_ = {}
            bufs = norm_bufs(psb, "e_")
            wsl = [psb("e_w%d" % i, [128, 16, 512], BF16) for i in range(3)]
            t_w = [Tok() for _ in range(3)]
            stg = [psb("e_stg%d" % i, [128, 512], BF16) for i in range(4)]
            t_stg = [Tok() for _ in range(4)]
            stf = [psb("e_stf%d" % i, [128, 512], F32) for i in range(2)]
            t_stf = [Tok() for _ in range(2)]
            a_st = psb("e_ast", [128, 4, 512], F32)
            t_ast = Tok()
            sig = [psb("e_sig%d" % i, [128, 512], F32) for i in range(2)]
            t_sig = [Tok() for _ in range(2)]
            zm = psb("e_zm", [128, 512], F32)
            t_zm = Tok()
            P.emit("sp", lambda e: e.dma_start(out=zm[:], in_=zmask_d.to_broadcast([128, 512])),
                   writes=[t_zm], dsem="e_zm")
            cnt = [0, 0, 0, 0]

            def loadw(g):
                s = cnt[0] % 3
                cnt[0] += 1
                P.emit("pool", lambda e, s=s, g=g: e.dma_start(out=wsl[s][:], in_=wr[:, :, g * 512:(g + 1) * 512]),
                       writes=[t_w[s]], dsem="e_w%d" % s)
                return s

            def featmajor(s, j):
                py = cnt[1] % 4
                cnt[1] += 1
                hT, t_hT = ch_["hT"], ch_["t_hT"]
                for kc in range(16):
                    P.emit("pe", lambda e, s=s, j=j, kc=kc, py=py, hT=hT: e.matmul(
                        PS[py][:], lhsT=wsl[s][:, kc, j * 128:(j + 1) * 128], rhs=hT[:, kc, :],
                        start=(kc == 0), stop=(kc == 15)),
                        reads=[t_w[s], t_hT[kc]], writes=[PT[py]])
                return py

            def tokmajor(s, ti):
                py = cnt[1] % 4
                cnt[1] += 1
                hT, t_hT = ch_["hT"], ch_["t_hT"]
                for kc in range(16):
                    P.emit("pe", lambda e, s=s, ti=ti, kc=kc, py=py, hT=hT: e.matmul(
                        PS[py][:], lhsT=hT[:, kc, ti * 128:(ti + 1) * 128], rhs=wsl[s][:, kc, :],
                        start=(kc == 0), stop=(kc == 15)),
                        reads=[t_w[s], t_hT[kc]], writes=[PT[py]])
                return py

            def to_bf(py, eng):
                i = cnt[2] % 4
                cnt[2] += 1
                if eng == "act":
                    P.emit("act", lambda e, i=i, py=py: e.activation(out=stg[i][:], in_=PS[py][:], func=AF.Copy),
                           reads=[PT[py]], writes=[t_stg[i]])
                else:
                    P.emit("dve", lambda e, i=i, py=py: e.tensor_copy(out=stg[i][:], in_=PS[py][:]),
                           reads=[PT[py]], writes=[t_stg[i]])
                return i

            norm_block(psb, 0, 0, 1, xT, t_xT, hT2[0], t_hT2[0], bufs)
            for b in range(6):
                ch_["hT"], ch_["t_hT"] = hT2[b % 2], t_hT2[b % 2]
                t0 = b * 512
                for g in range(6):
                    if g == 3 and b + 1 < 6:
                        norm_block(psb, b + 1, 0, 1, xT, t_xT, hT2[(b + 1) % 2], t_hT2[(b + 1) % 2], bufs)
                    if g < 2 and b == 5:
                        continue
                    s = loadw(g)
                    if g < 4:
                        for j in range(4):
                            h = (g % 2) * 4 + j
                            py = featmajor(s, j)
                            i = to_bf(py, "act" if j % 2 else "dve")
                            if g < 2:
                                P.emit("sp", lambda e, i=i, h=h, t0=t0: e.dma_start(out=qT_s[h, :, t0:t0 + 512], in_=stg[i][:]),
                                       reads=[t_stg[i]], writes=[tk_q], dsem="e_s%d" % i)
                            else:
                                P.emit("sp", lambda e, i=i, h=h, t0=t0: e.dma_start(out=kT_s[h, :, t0:t0 + 512], in_=stg[i][:]),
                                       reads=[t_stg[i]], writes=[tk_k], dsem="e_s%d" % i)
                    if (g in (2, 3) and b == 0) or g in (4, 5):
                        for ti in range(4):
                            py = tokmajor(s, ti)
                            r0 = t0 + ti * 128
                            if b == 0:
                                f = cnt[3] % 2
                                cnt[3] += 1
                                P.emit("dve", lambda e, f=f, py=py: e.tensor_copy(out=stf[f][:], in_=PS[py][:]),
                                       reads=[PT[py]], writes=[t_stf[f]])
                                od = nak_d if g < 4 else nav_d
                                c0 = (g % 2) * 512
                                P.emit("sp", lambda e, f=f, od=od, r0=r0, c0=c0: e.dma_start(
                                    out=od[r0:r0 + 128, c0:c0 + 512], in_=stf[f][:]),
                                    reads=[t_stf[f]], dsem="e_f%d" % f)
                            if g in (4, 5):
                                if b == 0:
                                    i = cnt[2] % 4
                                    cnt[2] += 1
                                    P.emit("act", lambda e, i=i, f=f: e.activation(out=stg[i][:], in_=stf[f][:], func=AF.Copy),
                                           reads=[t_stf[f]], writes=[t_stg[i]])
                                else:
                                    i = to_bf(py, "act" if ti % 2 else "dve")
                                c0 = (g - 4) * 512
                                P.emit("sp", lambda e, i=i, r0=r0, c0=c0: e.dma_start(
                                    out=v_s[r0:r0 + 128, c0:c0 + 512], in_=stg[i][:]),
                                    reads=[t_stg[i]], writes=[tk_v], dsem="e_s%d" % i)
                for hf in range(2):
                    s = loadw(6 + hf)
                    for j in range(4):
                        py = featmajor(s, j)
                        P.emit("dve", lambda e, j=j, py=py: e.tensor_copy(out=a_st[:, j, :], in_=PS[py][:]),
                               reads=[PT[py]], writes=[t_ast])
                    s = loadw(8 + hf)
                    for j in range(4):
                        ch = hf * 4 + j
                        py = featmajor(s, j)
                        P.emit("act", lambda e, j=j, py=py: e.activation(out=sig[j % 2][:], in_=PS[py][:], func=AF.Sigmoid),
                               reads=[PT[py]], writes=[t_sig[j % 2]])
                        if b == 5:
                            P.emit("dve", lambda e, j=j: e.tensor_tensor(out=sig[j % 2][:], in0=sig[j % 2][:], in1=zm[:], op=ALU.mult),
                                   reads=[t_sig[j % 2], t_zm], writes=[t_sig[j % 2]])
                        i = cnt[2] % 4
                        cnt[2] += 1
                        P.emit("dve", lambda e, j=j, i=i: e.tensor_tensor(out=stg[i][:], in0=a_st[:, j, :], in1=sig[j % 2][:], op=ALU.mult),
                               reads=[t_ast, t_sig[j % 2]], writes=[t_stg[i]])
                        P.emit("sp", lambda e, i=i, ch=ch, t0=t0: e.dma_start(out=zT_s[ch, :, t0:t0 + 512], in_=stg[i][:]),
                               reads=[t_stg[i]], writes=[tk_z], dsem="e_s%d" % i)
        P.barrier()

    def tokoff(r):
        if r < 4:
            return 2560 + r * 64
        if r < 36:
            return 512 + (r - 4) * 64
        return 2816 + (r - 36) * 64

    def even_attn():
        with ExitStack() as ph:
            def psb(name, shape, dt):
                return ph.enter_context(nc.sbuf_tensor(name + _uid(), list(shape), dt))
            kT = [psb("a_kT%d" % i, [128, NTA], BF16) for i in range(2)]
            qT = [psb("a_qT%d" % i, [128, NTO], BF16) for i in range(2)]
            V = [psb("a_V%d" % i, [128, 24, 129], BF16) for i in range(2)]
            cV = [psb("a_cV%d" % i, [128, 4, 129], BF16) for i in range(2)]
            ckf = [psb("a_ckf%d" % i, [128, 4, 128], F32) for i in range(2)]
            ckT = [psb("a_ckT%d" % i, [128, 512], BF16) for i in range(2)]
            nb = [psb("a_nb%d" % i, [128, 5 * 768], F32) for i in range(2)]
            t_h = [Tok() for _ in range(2)]
            t_ckf = [Tok() for _ in range(2)]
            t_ckT = [Tok() for _ in range(2)]
            oT = [psb("a_oT%d" % i, [128, NTO], BF16) for i in range(2)]
            t_oT = [Tok() for _ in range(2)]
            sl = [psb("a_sl%d" % i, [128, 768], F32) for i in range(2)]
            t_sl = [Tok() for _ in range(2)]
            el = [psb("a_el%d" % i, [128, 6, 128], BF16) for i in range(2)]
            t_el = [Tok() for _ in range(2)]
            ec = [psb("a_ec%d" % i, [128, 4, 128], BF16) for i in range(2)]
            t_ec = [Tok() for _ in range(2)]
            ep = [psb("a_ep%d" % i, [128, 2, 256], BF16) for i in range(2)]
            t_ep = [Tok() for _ in range(2)]
            rz = [psb("a_rz%d" % i, [128, 1], F32) for i in range(2)]
            on = [psb("a_on%d" % i, [128, 128], F32) for i in range(2)]
            t_on = [Tok() for _ in range(2)]
            for i in range(2):
                P.emit("dve", lambda e, i=i: e.memset(V[i][:, :, 128:129], 1.0), writes=[t_h[i]])
                P.emit("dve", lambda e, i=i: e.memset(cV[i][:, :, 128:129], 1.0), writes=[t_h[i]])
            cnt = [0]

            def fin1(po, k):
                P.emit("dve", lambda e, k=k, po=po: e.reciprocal(out=rz[k][:], in_=PS[po][:, 128:129]),
                       reads=[PT[po]], writes=[t_on[k]])
                P.emit("dve", lambda e, k=k, po=po: e.tensor_scalar(out=on[k][:], in0=PS[po][:, 0:128], scalar1=rz[k][:, 0:1],
                                                                    scalar2=None, op0=ALU.mult),
                       reads=[PT[po], t_on[k]], writes=[t_on[k]])

            def fin2(po, k, hb, q0):
                P.emit("pe", lambda e, k=k, po=po: e.transpose(out=PS[po][:, 256:384], in_=on[k][:], identity=ident[:]),
                       reads=[t_on[k], t_const], writes=[PT[po]])
                P.emit("act", lambda e, po=po, hb=hb, q0=q0: e.activation(out=oT[hb][:, q0:q0 + 128], in_=PS[po][:, 256:384], func=AF.Copy),
                       reads=[PT[po]], writes=[t_oT[hb]])

            def finish(po, hb, q0):
                k = cnt[0] % 2
                cnt[0] += 1
                fin1(po, k)
                fin2(po, k, hb, q0)

            def loads(h):
                hb = h % 2
                P.emit("sp", lambda e, h=h, hb=hb: e.dma_start(out=kT[hb][:], in_=kT_s[h]), reads=[tk_k], writes=[t_h[hb]], dsem="a_l%d" % hb)
                P.emit("sp", lambda e, h=h, hb=hb: e.dma_start(out=qT[hb][:], in_=qT_s[h]), reads=[tk_q], writes=[t_h[hb]], dsem="a_l%d" % hb)
                P.emit("sp", lambda e, h=h, hb=hb: e.dma_start(
                    out=V[hb][:, :, 0:128], in_=v_s[:, h * 128:(h + 1) * 128].rearrange("(t p) d -> p t d", p=128)),
                    reads=[tk_v], writes=[t_h[hb]], dsem="a_l%d" % hb)
                P.emit("sp", lambda e, h=h, hb=hb: e.dma_start(out=nb[hb][:], in_=nb_d[h]), writes=[t_h[hb]], dsem="a_l%d" % hb)
                P.emit("pool", lambda e, h=h, hb=hb: e.dma_start(
                    out=cV[hb][:, :, 0:128], in_=cv_a[h].rearrange("(t p) d -> p t d", p=128)),
                    writes=[t_h[hb]], dsem="a_c%d" % hb)
                P.emit("sp", lambda e, h=h, hb=hb: e.dma_start(out=ckf[hb][:], in_=ck_a[h].rearrange("(t p) d -> p t d", p=128)),
                       writes=[t_ckf[hb]], dsem="a_k%d" % hb)
                for t in range(4):
                    P.emit("pe", lambda e, t=t, hb=hb: e.transpose(out=PS[6][:, t * 128:(t + 1) * 128], in_=ckf[hb][:, t, :], identity=ident[:]),
                           reads=[t_ckf[hb], t_const], writes=[PT[6]])
                P.emit("dve", lambda e, hb=hb: e.tensor_copy(out=ckT[hb][:], in_=PS[6][:]), reads=[PT[6]], writes=[t_ckT[hb]])

            loads(0)
            for h in range(8):
                hb = h % 2
                if h + 1 < 8:
                    loads(h + 1)
                for s_ in range(2):
                    pb = s_ % 2
                    for kt in range(2):
                        P.emit("pe", lambda e, s_=s_, kt=kt, hb=hb, pb=pb: e.matmul(
                            PS[pb][:, kt * 256:(kt + 1) * 256], lhsT=kT[hb][:, s_ * 256 + kt * 128:s_ * 256 + (kt + 1) * 128],
                            rhs=qT[hb][:, s_ * 256:(s_ + 1) * 256], start=True, stop=True),
                            reads=[t_h[hb]], writes=[PT[pb]])
                    P.emit("act", lambda e, pb=pb: e.activation(out=ep[pb][:].rearrange("p a b -> p (a b)"), in_=PS[pb][:], func=AF.Exp, scale=SC_A),
                           reads=[PT[pb]], writes=[t_ep[pb]])
                    for qt in range(2):
                        po = 3 if qt == 0 else 7
                        for kt in range(2):
                            P.emit("pe", lambda e, qt=qt, kt=kt, pb=pb, hb=hb, s_=s_, po=po: e.matmul(
                                PS[po][:, 0:129], lhsT=ep[pb][:, kt, qt * 128:(qt + 1) * 128], rhs=V[hb][:, s_ * 2 + kt, :],
                                start=(kt == 0), stop=(kt == 1)),
                                reads=[t_ep[pb], t_h[hb]], writes=[PT[po]])
                        finish(po, hb, s_ * 256 + qt * 128)
                def banks(p):
                    return ((0, 1, 2, 3) if p % 2 == 0 else (4, 5, 6, 7))

                def qk(p):
                    ws = min(max(2 * p, 0), 28)
                    q0 = 512 + p * 128
                    bl0, bl1, bc, _ = banks(p)
                    for t in range(6):
                        ko = tokoff(ws + 2 * t)
                        bank, col = (bl0, t) if t < 4 else (bl1, t - 4)
                        P.emit("pe", lambda e, ko=ko, bank=bank, col=col, hb=hb, q0=q0: e.matmul(
                            PS[bank][:, col * 128:(col + 1) * 128], lhsT=kT[hb][:, ko:ko + 128], rhs=qT[hb][:, q0:q0 + 128],
                            start=True, stop=True),
                            reads=[t_h[hb]], writes=[PT[bank]])
                    for t in range(4):
                        P.emit("pe", lambda e, t=t, hb=hb, q0=q0, bc=bc: e.matmul(
                            PS[bc][:, t * 128:(t + 1) * 128], lhsT=ckT[hb][:, t * 128:(t + 1) * 128], rhs=qT[hb][:, q0:q0 + 128],
                            start=True, stop=True),
                            reads=[t_ckT[hb], t_h[hb]], writes=[PT[bc]])

                def mid(p):
                    var = {0: 1, 1: 2, 14: 3, 15: 4}.get(p, 0)
                    k = p % 2
                    bl0, bl1, bc, _ = banks(p)
                    P.emit("dve", lambda e, k=k, hb=hb, var=var, bl0=bl0: e.scalar_tensor_tensor(
                        out=sl[k][:, 0:512], in0=PS[bl0][:], scalar=SC_A, in1=nb[hb][:, var * 768:var * 768 + 512],
                        op0=ALU.mult, op1=ALU.add), reads=[PT[bl0], t_h[hb]], writes=[t_sl[k]])
                    P.emit("dve", lambda e, k=k, hb=hb, var=var, bl1=bl1: e.scalar_tensor_tensor(
                        out=sl[k][:, 512:768], in0=PS[bl1][:, 0:256], scalar=SC_A, in1=nb[hb][:, var * 768 + 512:var * 768 + 768],
                        op0=ALU.mult, op1=ALU.add), reads=[PT[bl1], t_h[hb]], writes=[t_sl[k]])
                    P.emit("act", lambda e, k=k, bc=bc: e.activation(out=ec[k][:].rearrange("p a b -> p (a b)"), in_=PS[bc][:], func=AF.Exp, scale=SC_A),
                           reads=[PT[bc]], writes=[t_ec[k]])
                    P.emit("act", lambda e, k=k: e.activation(out=el[k][:].rearrange("p a b -> p (a b)"), in_=sl[k][:], func=AF.Exp),
                           reads=[t_sl[k]], writes=[t_el[k]])

                def pv(p):
                    ws = min(max(2 * p, 0), 28)
                    k = p % 2
                    po = banks(p)[3]
                    for t in range(4):
                        P.emit("pe", lambda e, t=t, k=k, hb=hb, po=po: e.matmul(
                            PS[po][:, 0:129], lhsT=ec[k][:, t, :], rhs=cV[hb][:, t, :], start=(t == 0), stop=False),
                            reads=[t_ec[k], t_h[hb]], writes=[PT[po]])
                    for t in range(6):
                        vt = tokoff(ws + 2 * t) // 128
                        P.emit("pe", lambda e, t=t, k=k, hb=hb, po=po, vt=vt: e.matmul(
                            PS[po][:, 0:129], lhsT=el[k][:, t, :], rhs=V[hb][:, vt, :], start=False, stop=(t == 5)),
                            reads=[t_el[k], t_h[hb]], writes=[PT[po]])

                qk(0)
                for p in range(16):
                    mid(p)
                    if p + 1 < 16:
                        qk(p + 1)
                    pv(p)
                    fin1(banks(p)[3], p % 2)
                    if p >= 1:
                        fin2(banks(p - 1)[3], (p - 1) % 2, hb, 512 + (p - 1) * 128)
                fin2(banks(15)[3], 1, hb, 512 + 15 * 128)
                P.emit("sp", lambda e, h=h, hb=hb: e.dma_start(out=mixT_s[h], in_=oT[hb][:]),
                       reads=[t_oT[hb]], writes=[tk_mix], dsem="a_o%d" % hb)
        P.barrier()

    def even_conv():
        with ExitStack() as ph:
            def psb(name, shape, dt):
                return ph.enter_context(nc.sbuf_tensor(name + _uid(), list(shape), dt))
            LS = 2560 + 30
            zp = psb("c_zp", [128, 8, LS], BF16)
            zq = psb("c_zq", [128, 8, 2, 286], BF16)
            t_z = Tok()
            dg = psb("c_dg", [128, 8 * 31, 128], BF16)
            dwT = psb("c_dwT", [128, 8 * 31], F32)
            cvp = psb("c_cvp", [128, 24], F32)
            idb = psb("c_idb", [128, 128], BF16)
            onesf = psb("c_onesf", [128, 128], F32)
            t_c = Tok()
            cvb = psb("c_cv", [128, 8, 512], F32)
            t_cv = [Tok() for _ in range(8)]
            sqf = [psb("c_sqf%d" % i, [128, 512], F32) for i in range(2)]
            t_sqf = [Tok() for _ in range(2)]
            mean = psb("c_mean", [128, 512], F32)
            rstd = psb("c_rstd", [128, 512], F32)
            t_st = Tok()
            tmp = [psb("c_tmp%d" % i, [128, 512], F32) for i in range(2)]
            t_tmp = [Tok() for _ in range(2)]
            stg = [psb("c_stg%d" % i, [128, 512], BF16) for i in range(2)]
            t_stg = [Tok() for _ in range(2)]
            P.emit("sp", lambda e: e.dma_start(out=dwT[:], in_=dwT_d), writes=[t_c], dsem="c_c")
            P.emit("sp", lambda e: e.dma_start(out=cvp[:], in_=cvp_d), writes=[t_c], dsem="c_c")
            P.emit("dve", lambda e: e.tensor_copy(out=idb[:], in_=ident[:]), reads=[t_const], writes=[t_c])
            P.emit("dve", lambda e: e.memset(onesf[:], 1.0), writes=[t_c])
            P.emit("dve", lambda e: e.memset(zp[:, :, 0:15], 0.0), writes=[t_z])
            P.emit("dve", lambda e: e.memset(zp[:, :, LS - 15:LS], 0.0), writes=[t_z])
            P.emit("dve", lambda e: e.memset(zq[:].rearrange("p a b c -> p (a b c)"), 0.0), writes=[t_z])
            for ch in range(8):
                P.emit("sp", lambda e, ch=ch: e.dma_start(out=zp[:, ch, 15:15 + 256], in_=zT_s[ch, :, 2560:2816]),
                       reads=[tk_z], writes=[t_z], dsem="c_z")
                P.emit("sp", lambda e, ch=ch: e.dma_start(out=zp[:, ch, 15 + 256:15 + 2304], in_=zT_s[ch, :, 512:2560]),
                       reads=[tk_z], writes=[t_z], dsem="c_z")
                P.emit("sp", lambda e, ch=ch: e.dma_start(out=zp[:, ch, 15 + 2304:15 + 2560], in_=zT_s[ch, :, 2816:3072]),
                       reads=[tk_z], writes=[t_z], dsem="c_z")
                for s_ in range(2):
                    P.emit("sp", lambda e, ch=ch, s_=s_: e.dma_start(out=zq[:, ch, s_, 15:15 + 256], in_=zT_s[ch, :, s_ * 256:(s_ + 1) * 256]),
                           reads=[tk_z], writes=[t_z], dsem="c_z")
                for j in range(31):
                    P.emit("dve", lambda e, ch=ch, j=j: e.tensor_scalar(
                        out=dg[:, ch * 31 + j, :], in0=idb[:], scalar1=dwT[:, ch * 31 + j:ch * 31 + j + 1], scalar2=None, op0=ALU.mult),
                        reads=[t_c], writes=[t_c])
            for pc in range(5):
                for ch in range(8):
                    py = ch % 4
                    if pc == 0:
                        for s_ in range(2):
                            for j in range(31):
                                P.emit("pe", lambda e, ch=ch, j=j, s_=s_, py=py: e.matmul(
                                    PS[py][:, s_ * 256:(s_ + 1) * 256], lhsT=dg[:, ch * 31 + j, :], rhs=zq[:, ch, s_, j:j + 256],
                                    start=(j == 0 and s_ == 0), stop=(j == 30 and s_ == 1)),
                                    reads=[t_c, t_z], writes=[PT[py]])
                    else:
                        o0 = 256 + (pc - 1) * 512
                        for j in range(31):
                            P.emit("pe", lambda e, ch=ch, j=j, o0=o0, py=py: e.matmul(
                                PS[py][:], lhsT=dg[:, ch * 31 + j, :], rhs=zp[:, ch, o0 + j:o0 + j + 512],
                                start=(j == 0), stop=(j == 30)),
                                reads=[t_c, t_z], writes=[PT[py]])
                    P.emit("act", lambda e, ch=ch, py=py: e.activation(out=cvb[:, ch, :], in_=PS[py][:], func=AF.Identity,
                                                                       bias=cvp[:, ch:ch + 1], scale=1.0),
                           reads=[PT[py], t_c], writes=[t_cv[ch]])
                    P.emit("pe", lambda e, ch=ch: e.matmul(PS[4][:], lhsT=onesf[:], rhs=cvb[:, ch, :], start=(ch == 0), stop=(ch == 7)),
                           reads=[t_cv[ch], t_c], writes=[PT[4]])
                    P.emit("act", lambda e, ch=ch: e.activation(out=sqf[ch % 2][:], in_=cvb[:, ch, :], func=AF.Square),
                           reads=[t_cv[ch]], writes=[t_sqf[ch % 2]])
                    P.emit("pe", lambda e, ch=ch: e.matmul(PS[5][:], lhsT=onesf[:], rhs=sqf[ch % 2][:], start=(ch == 0), stop=(ch == 7)),
                           reads=[t_sqf[ch % 2], t_c], writes=[PT[5]])
                P.emit("dve", lambda e: e.tensor_scalar(out=mean[:], in0=PS[4][:], scalar1=1.0 / 1024, scalar2=None, op0=ALU.mult),
                       reads=[PT[4]], writes=[t_st])
                P.emit("dve", lambda e: e.tensor_tensor(out=rstd[:], in0=mean[:], in1=mean[:], op=ALU.mult), reads=[t_st], writes=[t_st])
                P.emit("dve", lambda e: e.scalar_tensor_tensor(out=rstd[:], in0=PS[5][:], scalar=1.0 / 1024, in1=rstd[:],
                                                               op0=ALU.mult, op1=ALU.subtract), reads=[PT[5], t_st], writes=[t_st])
                P.emit("dve", lambda e: e.tensor_scalar(out=rstd[:], in0=rstd[:], scalar1=EPS, scalar2=None, op0=ALU.add),
                       reads=[t_st], writes=[t_st])
                P.emit("act", lambda e: e.activation(out=rstd[:], in_=rstd[:], func=AF.Sqrt), reads=[t_st], writes=[t_st])
                P.emit("dve", lambda e: e.reciprocal(out=rstd[:], in_=rstd[:]), reads=[t_st], writes=[t_st])
                t0 = pc * 512
                for ch in range(8):
                    k = ch % 2
                    P.emit("dve", lambda e, ch=ch, k=k: e.tensor_tensor(out=tmp[k][:], in0=cvb[:, ch, :], in1=mean[:], op=ALU.subtract),
                           reads=[t_cv[ch], t_st], writes=[t_tmp[k]])
                    P.emit("dve", lambda e, k=k: e.tensor_tensor(out=tmp[k][:], in0=tmp[k][:], in1=rstd[:], op=ALU.mult),
                           reads=[t_tmp[k], t_st], writes=[t_tmp[k]])
                    P.emit("act", lambda e, ch=ch, k=k: e.activation(out=stg[k][:], in_=tmp[k][:], func=AF.Silu,
                                                                     bias=cvp[:, 16 + ch:17 + ch], scale=cvp[:, 8 + ch:9 + ch]),
                           reads=[t_tmp[k], t_c], writes=[t_stg[k]])
                    P.emit("sp", lambda e, ch=ch, k=k, t0=t0: e.dma_start(out=mixT_s[8 + ch, :, t0:t0 + 512], in_=stg[k][:]),
                           reads=[t_stg[k]], writes=[tk_mix], dsem="c_s%d" % k)
        P.barrier()

    c_w_in = din("c_w_in", [D, 6144])
    c_w_out = din("c_w_out", [D, D])
    ck_c = din("cache_c_k", [16, 512, 128])
    cv_c = din("cache_c_v", [8, 512, 256])
    lamb_d = din("lamb", [1, 512])
    sg_d = din("subg", [1, 256])
    cos_d = din("ropec", [128, 2048])
    sin_d = din("ropes", [128, 2048])
    perm_d = din("perm_in", [128, 128])
    nck_d = dout("nck", [512, 2048])
    ncv_d = dout("ncv", [512, 2048])
    q1T_s = dscr("q1T_s", [16, 128, NTO], BF16, out=dev)
    kpT_s = dscr("kpT_s", [16, 128, 512], BF16)
    vp_s = dscr("vp_s", [512, 2048], BF16)
    kx = [dscr("kx%d" % i, [256, 2048], BF16) for i in range(8)]
    vx = [dscr("vx%d" % i, [256, 2048], BF16) for i in range(8)]
    kxa = [dscr("kxa%d" % i, [512, 2048], BF16) for i in range(8)]
    vxa = [dscr("vxa%d" % i, [512, 2048], BF16) for i in range(8)]
    tk_q1, tk_kp, tk_vp, tk_kx, tk_vx, tk_kxa, tk_vxa = [Tok() for _ in range(7)]
    LAM_INIT = 0.8 - 0.6 * math.exp(-0.3 * 1)
    neglam = sb("neglam", [128, 1], F32)
    sgt = sb("sgt", [128, 256], F32)
    t_lam = Tok()

    def odd_setup():
        with ExitStack() as ph:
            def psb(name, shape, dt):
                return ph.enter_context(nc.sbuf_tensor(name + _uid(), list(shape), dt))
            lb = psb("l_lb", [128, 512], F32)
            pr = psb("l_pr", [128, 256], F32)
            e2 = psb("l_e2", [128, 2], F32)
            P.emit("sp", lambda e: e.dma_start(out=lb[:], in_=lamb_d.to_broadcast([128, 512])), writes=[t_lam], dsem="l_l")
            P.emit("sp", lambda e: e.dma_start(out=sgt[:], in_=sg_d.to_broadcast([128, 256])), writes=[t_lam], dsem="l_l")
            P.emit("dve", lambda e: e.tensor_tensor(out=pr[:, 0:128], in0=lb[:, 0:128], in1=lb[:, 128:256], op=ALU.mult), reads=[t_lam], writes=[t_lam])
            P.emit("dve", lambda e: e.tensor_tensor(out=pr[:, 128:256], in0=lb[:, 256:384], in1=lb[:, 384:512], op=ALU.mult), reads=[t_lam], writes=[t_lam])
            P.emit("dve", lambda e: e.tensor_reduce(out=e2[:], in_=pr[:].rearrange("p (a b) -> p a b", a=2), axis=mybir.AxisListType.X, op=ALU.add),
                   reads=[t_lam], writes=[t_lam])
            P.emit("act", lambda e: e.activation(out=e2[:], in_=e2[:], func=AF.Exp), reads=[t_lam], writes=[t_lam])
            P.emit("dve", lambda e: e.tensor_tensor(out=neglam[:], in0=e2[:, 1:2], in1=e2[:, 0:1], op=ALU.subtract), reads=[t_lam], writes=[t_lam])
            P.emit("dve", lambda e: e.tensor_scalar(out=neglam[:], in0=neglam[:], scalar1=-LAM_INIT, scalar2=None, op0=ALU.add), reads=[t_lam], writes=[t_lam])
            P.emit("dve", lambda e: e.tensor_scalar(out=sgt[:], in0=sgt[:], scalar1=1.0 - LAM_INIT, scalar2=None, op0=ALU.mult), reads=[t_lam], writes=[t_lam])
        P.barrier()

    def odd_proj():
        wr = c_w_in.rearrange("(k p) n -> p k n", p=128)
        with ExitStack() as ph:
            def psb(name, shape, dt):
                return ph.enter_context(nc.sbuf_tensor(name + _uid(), list(shape), dt))
            xT = psb("p_xT", [128, 16, 512], F32)
            t_xT = [Tok() for _ in range(16)]
            hT2 = [psb("p_hT", [128, 16, 512], BF16) for _ in range(2)]
            t_hT2 = [[Tok() for _ in range(16)] for _ in range(2)]
            bufs = norm_bufs(psb, "p_")
            wsl = [psb("p_w%d" % i, [128, 16, 512], BF16) for i in range(3)]
            t_w = [Tok() for _ in range(3)]
            stg = [psb("p_stg%d" % i, [128, 512], BF16) for i in range(4)]
            t_stg = [Tok() for _ in range(4)]
            stf = [psb("p_stf%d" % i, [128, 512], F32) for i in range(2)]
            t_stf = [Tok() for _ in range(2)]
            qf = [psb("p_qf%d" % i, [128, 512], F32) for i in range(2)]
            t_qf = [Tok() for _ in range(2)]
            r1 = [psb("p_r1%d" % i, [128, 512], F32) for i in range(2)]
            t_r1 = [Tok() for _ in range(2)]
            cosT = psb("p_cos", [128, 2048], F32)
            sinT = psb("p_sin", [128, 2048], F32)
            perm = psb("p_perm", [128, 128], F32)
            t_rp = Tok()
            P.emit("sp", lambda e: e.dma_start(out=cosT[:], in_=cos_d), writes=[t_rp], dsem="p_c")
            P.emit("sp", lambda e: e.dma_start(out=sinT[:], in_=sin_d), writes=[t_rp], dsem="p_c")
            P.emit("sp", lambda e: e.dma_start(out=perm[:], in_=perm_d), writes=[t_rp], dsem="p_c")
            cnt = [0, 0, 0, 0, 0]
            pend = [None]

            def loadw(g):
                s = cnt[0] % 3
                cnt[0] += 1
                P.emit("pool", lambda e, s=s, g=g: e.dma_start(out=wsl[s][:], in_=wr[:, :, g * 512:(g + 1) * 512]),
                       writes=[t_w[s]], dsem="p_w%d" % s)
                return s

            norm_block(psb, 0, 1, 1, xT, t_xT, hT2[0], t_hT2[0], bufs)
            for b in range(5):
                hT, t_hT = hT2[b % 2], t_hT2[b % 2]
                t0 = b * 512
                for g in range(12):
                    if g == 6 and b + 1 < 5:
                        norm_block(psb, b + 1, 1, 1, xT, t_xT, hT2[(b + 1) % 2], t_hT2[(b + 1) % 2], bufs)
                    s = loadw(g)
                    if g < 8:
                        for j in range(4):
                            mp = (g % 4) * 4 + j
                            py = cnt[1] % 4
                            cnt[1] += 1
                            for kc in range(16):
                                P.emit("pe", lambda e, s=s, j=j, kc=kc, py=py, hT=hT: e.matmul(
                                    PS[py][:], lhsT=wsl[s][:, kc, j * 128:(j + 1) * 128], rhs=hT[:, kc, :],
                                    start=(kc == 0), stop=(kc == 15)),
                                    reads=[t_w[s], t_hT[kc]], writes=[PT[py]])
                            def store(i, g=g, mp=mp, t0=t0, b=b):
                                if g < 4:
                                    P.emit("sp", lambda e, i=i, mp=mp, t0=t0: e.dma_start(out=q1T_s[mp, :, t0:t0 + 512], in_=stg[i][:]),
                                           reads=[t_stg[i]], writes=[tk_q1], dsem="p_s%d" % i)
                                elif b == 0:
                                    P.emit("sp", lambda e, i=i, mp=mp: e.dma_start(out=kpT_s[mp], in_=stg[i][:]),
                                           reads=[t_stg[i]], writes=[tk_kp], dsem="p_s%d" % i)
                                else:
                                    o0 = (b - 1) * 512
                                    P.emit("sp", lambda e, i=i, mp=mp, o0=o0: e.dma_start(out=kx[mp // 2][(mp % 2) * 128:(mp % 2 + 1) * 128, o0:o0 + 512], in_=stg[i][:]),
                                           reads=[t_stg[i]], writes=[tk_kx], dsem="p_s%d" % i)
                            if b == 0:
                                i = cnt[2] % 4
                                cnt[2] += 1
                                P.emit("act", lambda e, i=i, py=py: e.activation(out=stg[i][:], in_=PS[py][:], func=AF.Copy),
                                       reads=[PT[py]], writes=[t_stg[i]])
                                store(i)
                            else:
                                k = cnt[4] % 2
                                cnt[4] += 1
                                P.emit("act", lambda e, k=k, py=py: e.activation(out=qf[k][:], in_=PS[py][:], func=AF.Copy),
                                       reads=[PT[py]], writes=[t_qf[k]])

                                def rest(k=k, o0=(b - 1) * 512, store=store):
                                    i = cnt[2] % 4
                                    cnt[2] += 1
                                    P.emit("pe", lambda e, k=k: e.matmul(PS[4 + k][:], lhsT=perm[:], rhs=qf[k][:], start=True, stop=True),
                                           reads=[t_qf[k], t_rp], writes=[PT[4 + k]])
                                    P.emit("dve", lambda e, k=k, o0=o0: e.tensor_tensor(out=r1[k][:], in0=qf[k][:], in1=cosT[:, o0:o0 + 512], op=ALU.mult),
                                           reads=[t_qf[k], t_rp], writes=[t_r1[k]])
                                    P.emit("dve", lambda e, k=k, o0=o0: e.tensor_tensor(out=qf[k][:], in0=PS[4 + k][:], in1=sinT[:, o0:o0 + 512], op=ALU.mult),
                                           reads=[PT[4 + k], t_rp, t_r1[k]], writes=[t_qf[k]])
                                    P.emit("dve", lambda e, k=k, i=i: e.tensor_tensor(out=stg[i][:], in0=qf[k][:], in1=r1[k][:], op=ALU.add),
                                           reads=[t_qf[k], t_r1[k]], writes=[t_stg[i]])
                                    store(i)
                                prev = pend[0]
                                pend[0] = rest
                                if prev is not None:
                                    prev()
                        if pend[0] is not None:
                            pend[0]()
                            pend[0] = None
                    if (g >= 4 and g < 8 and b == 0) or g >= 8:
                        for ti in range(4):
                            py = cnt[1] % 4
                            cnt[1] += 1
                            for kc in range(16):
                                P.emit("pe", lambda e, s=s, ti=ti, kc=kc, py=py, hT=hT: e.matmul(
                                    PS[py][:], lhsT=hT[:, kc, ti * 128:(ti + 1) * 128], rhs=wsl[s][:, kc, :],
                                    start=(kc == 0), stop=(kc == 15)),
                                    reads=[t_w[s], t_hT[kc]], writes=[PT[py]])
                            c0 = (g % 4) * 512
                            if b == 0:
                                f = cnt[3] % 2
                                cnt[3] += 1
                                r0 = ti * 128
                                P.emit("dve", lambda e, f=f, py=py: e.tensor_copy(out=stf[f][:], in_=PS[py][:]),
                                       reads=[PT[py]], writes=[t_stf[f]])
                                od = nck_d if g < 8 else ncv_d
                                P.emit("sp", lambda e, f=f, od=od, r0=r0, c0=c0: e.dma_start(out=od[r0:r0 + 128, c0:c0 + 512], in_=stf[f][:]),
                                       reads=[t_stf[f]], dsem="p_f%d" % f)
                                if g >= 8:
                                    i = cnt[2] % 4
                                    cnt[2] += 1
                                    P.emit("act", lambda e, i=i, f=f: e.activation(out=stg[i][:], in_=stf[f][:], func=AF.Copy),
                                           reads=[t_stf[f]], writes=[t_stg[i]])
                                    P.emit("sp", lambda e, i=i, r0=r0, c0=c0: e.dma_start(out=vp_s[r0:r0 + 128, c0:c0 + 512], in_=stg[i][:]),
                                           reads=[t_stg[i]], writes=[tk_vp], dsem="p_s%d" % i)
                            else:
                                i = cnt[2] % 4
                                cnt[2] += 1
                                r0 = (b - 1) * 512 + ti * 128
                                if ti % 2:
                                    P.emit("act", lambda e, i=i, py=py: e.activation(out=stg[i][:], in_=PS[py][:], func=AF.Copy),
                                           reads=[PT[py]], writes=[t_stg[i]])
                                else:
                                    P.emit("dve", lambda e, i=i, py=py: e.tensor_copy(out=stg[i][:], in_=PS[py][:]),
                                           reads=[PT[py]], writes=[t_stg[i]])
                                P.emit("sp", lambda e, i=i, r0=r0, c0=c0: e.dma_start(out=vx[r0 // 256][r0 % 256:r0 % 256 + 128, c0:c0 + 512], in_=stg[i][:]),
                                       reads=[t_stg[i]], writes=[tk_vx], dsem="p_s%d" % i)
        P.barrier()

    def odd_cc():
        groups = [[2 * i, 2 * i + 1] for i in range(ncores // 2)]
        for i in range(8):
            P.emit("pool", lambda e, i=i: e.collective_compute("AllGather", ALU.bypass, replica_groups=groups, ins=[kx[i]], outs=[kxa[i]]),
                   reads=[tk_kx], writes=[tk_kxa], dsem="cc_k", cc=True)
            P.emit("pool", lambda e, i=i: e.collective_compute("AllGather", ALU.bypass, replica_groups=groups, ins=[vx[i]], outs=[vxa[i]]),
                   reads=[tk_vx], writes=[tk_vxa], dsem="cc_v", cc=True)
        P.barrier()

    SC_C = 128 ** -0.5

    def odd_attn():
        with ExitStack() as ph:
            def psb(name, shape, dt):
                return ph.enter_context(nc.sbuf_tensor(name + _uid(), list(shape), dt))
            kT1_ = [psb("d_kT1", [128, 2, 4096], BF16) for _ in range(2)]
            q1T_ = [psb("d_q1T", [128, 2, NTO], BF16) for _ in range(2)]
            kpT_ = [psb("d_kpT", [128, 2, 512], BF16) for _ in range(2)]
            ckf_ = [psb("d_ckf", [128, 2, 4, 128], F32) for _ in range(2)]
            ckT_ = [psb("d_ckT", [128, 2, 512], BF16) for _ in range(2)]
            V1_ = [psb("d_V1", [128, 32, 257], BF16) for _ in range(2)]
            cV_ = [psb("d_cV", [128, 4, 257], BF16) for _ in range(2)]
            Vp_ = [psb("d_Vp", [128, 4, 257], BF16) for _ in range(2)]
            t_h_ = [Tok() for _ in range(2)]
            t_ckf_ = [Tok() for _ in range(2)]
            t_ckT_ = [Tok() for _ in range(2)]
            cur = {}
            eb = [psb("d_e%d" % i, [128, 512], BF16) for i in range(4)]
            t_e = [Tok() for _ in range(4)]
            On = [psb("d_On%d" % i, [128, 2, 4, 256], F32) for i in range(2)]
            t_On = [[[Tok() for _ in range(4)] for _ in range(2)] for _ in range(2)]
            rz = [psb("d_rz%d" % i, [128, 1], F32) for i in range(4)]
            t_rz = [Tok() for _ in range(4)]
            ob = [psb("d_ob%d" % i, [128, 4, 256], F32) for i in range(2)]
            t_ob = [[Tok() for _ in range(4)] for _ in range(2)]
            junk = psb("d_junk", [128, 256], F32)
            t_junk = Tok()
            ssq = [psb("d_ssq%d" % i, [128, 4], F32) for i in range(2)]
            t_ssq = [Tok() for _ in range(2)]
            oT1 = psb("d_oT", [128, 2, NTO], BF16)
            t_oT = Tok()
            for i in range(2):
                P.emit("dve", lambda e, i=i: e.memset(V1_[i][:, :, 256:257], 1.0), writes=[t_h_[i]])
                P.emit("dve", lambda e, i=i: e.memset(cV_[i][:, :, 256:257], 1.0), writes=[t_h_[i]])
                P.emit("dve", lambda e, i=i: e.memset(Vp_[i][:, :, 256:257], 1.0), writes=[t_h_[i]])
            cnt = [0, 0, 0]
            pending = [None]
            SB = [0, 1, 6]

            def flush():
                if pending[0] is not None:
                    f = pending[0]
                    pending[0] = None
                    f()

            def block(qap, ktiles, nq, q0):
                nqt = nq // 128
                nk = len(ktiles)
                bp = cnt[2] % 2
                cnt[2] += 1
                for m in range(2):
                    def S(kt, m=m):
                        pb = SB[kt % 3]
                        P.emit("pe", lambda e, kt=kt, pb=pb, m=m: e.matmul(
                            PS[pb][:, 0:nq], lhsT=ktiles[kt][0](m), rhs=qap(m), start=True, stop=True),
                            reads=[cur["t_h"], cur["t_ckT"]], writes=[PT[pb]])
                    S(0)
                    if nk > 1:
                        S(1)
                    for kt in range(nk):
                        if kt + 2 < nk:
                            S(kt + 2)
                        ei = cnt[0] % 4
                        cnt[0] += 1
                        pb = SB[kt % 3]
                        P.emit("act", lambda e, ei=ei, pb=pb: e.activation(out=eb[ei][:, 0:nq], in_=PS[pb][:, 0:nq], func=AF.Exp, scale=SC_C),
                               reads=[PT[pb]], writes=[t_e[ei]])
                        for qt in range(nqt):
                            P.emit("pe", lambda e, ei=ei, qt=qt, kt=kt: e.matmul(
                                PS[2 + qt][:, 0:257], lhsT=eb[ei][:, qt * 128:(qt + 1) * 128], rhs=ktiles[kt][1],
                                start=(kt == 0), stop=(kt == nk - 1)),
                                reads=[t_e[ei], cur["t_h"]], writes=[PT[2 + qt]])
                    for qt in range(nqt):
                        k = cnt[1] % 4
                        cnt[1] += 1
                        po = 2 + qt
                        P.emit("dve", lambda e, k=k, po=po: e.reciprocal(out=rz[k][:], in_=PS[po][:, 256:257]),
                               reads=[PT[po]], writes=[t_rz[k]])
                        P.emit("dve", lambda e, k=k, po=po, qt=qt, m=m: e.tensor_scalar(
                            out=On[bp][:, m, qt, :], in0=PS[po][:, 0:256], scalar1=rz[k][:, 0:1], scalar2=None, op0=ALU.mult),
                            reads=[PT[po], t_rz[k]], writes=[t_On[bp][m][qt]])
                    if m == 0:
                        flush()
                for qt in range(nqt):
                    P.emit("dve", lambda e, qt=qt: e.scalar_tensor_tensor(
                        out=ob[bp][:, qt, :], in0=On[bp][:, 1, qt, :], scalar=neglam[:, 0:1], in1=On[bp][:, 0, qt, :],
                        op0=ALU.mult, op1=ALU.add),
                        reads=[t_On[bp][0][qt], t_On[bp][1][qt], t_lam], writes=[t_ob[bp][qt]])
                    P.emit("dve", lambda e, qt=qt: e.scalar_tensor_tensor(
                        out=junk[:], in0=ob[bp][:, qt, :], scalar=1.0, in1=ob[bp][:, qt, :], op0=ALU.mult, op1=ALU.mult,
                        accum_out=ssq[bp][:, qt:qt + 1]),
                        reads=[t_ob[bp][qt]], writes=[t_junk, t_ssq[bp]])

                def tail(bp=bp, nqt=nqt, q0=q0):
                    P.emit("dve", lambda e: e.tensor_scalar(out=ssq[bp][:, 0:nqt], in0=ssq[bp][:, 0:nqt], scalar1=1.0 / 256, scalar2=EPS,
                                                            op0=ALU.mult, op1=ALU.add), reads=[t_ssq[bp]], writes=[t_ssq[bp]])
                    P.emit("act", lambda e: e.activation(out=ssq[bp][:, 0:nqt], in_=ssq[bp][:, 0:nqt], func=AF.Sqrt),
                           reads=[t_ssq[bp]], writes=[t_ssq[bp]])
                    P.emit("dve", lambda e: e.reciprocal(out=ssq[bp][:, 0:nqt], in_=ssq[bp][:, 0:nqt]), reads=[t_ssq[bp]], writes=[t_ssq[bp]])
                    for qt in range(nqt):
                        P.emit("dve", lambda e, qt=qt: e.scalar_tensor_tensor(
                            out=ob[bp][:, qt, :], in0=ob[bp][:, qt, :], scalar=ssq[bp][:, qt:qt + 1], in1=sgt[:], op0=ALU.mult, op1=ALU.mult),
                            reads=[t_ob[bp][qt], t_ssq[bp], t_lam], writes=[t_ob[bp][qt]])
                        for c in range(2):
                            P.emit("pe", lambda e, qt=qt, c=c: e.transpose(out=PS[7][:, c * 128:(c + 1) * 128],
                                                                           in_=ob[bp][:, qt, c * 128:(c + 1) * 128], identity=ident[:]),
                                   reads=[t_ob[bp][qt], t_const], writes=[PT[7]])
                        qq = q0 + qt * 128
                        P.emit("dve", lambda e, qq=qq: e.tensor_copy(out=oT1[:, :, qq:qq + 128],
                                                                     in_=PS[7][:, 0:256].rearrange("p (c t) -> p c t", c=2)),
                               reads=[PT[7]], writes=[t_oT])
                pending[0] = tail

            def loads(hd):
                hb = hd % 2
                kT1, q1T, kpT, ckf, V1, cV, Vp = kT1_[hb], q1T_[hb], kpT_[hb], ckf_[hb], V1_[hb], cV_[hb], Vp_[hb]
                t_h, t_ckf = t_h_[hb], t_ckf_[hb]
                for m in range(2):
                    mp = 2 * hd + m
                    for r in range(2):
                        P.emit("sp", lambda e, m=m, mp=mp, r=r: e.dma_start(
                            out=kT1[:, m, r * 2048:(r + 1) * 2048],
                            in_=kxa[mp // 2][r * 256 + (mp % 2) * 128:r * 256 + (mp % 2 + 1) * 128, :]),
                            reads=[tk_kxa], writes=[t_h], dsem="d_l%d" % hb)
                    P.emit("sp", lambda e, m=m, mp=mp: e.dma_start(out=q1T[:, m, :], in_=q1T_s[mp]), reads=[tk_q1], writes=[t_h], dsem="d_l%d" % hb)
                    P.emit("sp", lambda e, m=m, mp=mp: e.dma_start(out=kpT[:, m, :], in_=kpT_s[mp]), reads=[tk_kp], writes=[t_h], dsem="d_l%d" % hb)
                    P.emit("sp", lambda e, m=m, mp=mp: e.dma_start(out=ckf[:, m], in_=ck_c[mp].rearrange("(t p) d -> p t d", p=128)),
                           writes=[t_ckf], dsem="d_k%d" % hb)
                for r in range(2):
                    for i in range(8):
                        P.emit("sp", lambda e, hd=hd, r=r, i=i: e.dma_start(
                            out=V1[:, r * 16 + 2 * i:r * 16 + 2 * i + 2, 0:256],
                            in_=vxa[i][r * 256:(r + 1) * 256, hd * 256:(hd + 1) * 256].rearrange("(t p) d -> p t d", p=128)),
                            reads=[tk_vxa], writes=[t_h], dsem="d_l%d" % hb)
                P.emit("sp", lambda e, hd=hd: e.dma_start(
                    out=Vp[:, :, 0:256], in_=vp_s[:, hd * 256:(hd + 1) * 256].rearrange("(t p) d -> p t d", p=128)),
                    reads=[tk_vp], writes=[t_h], dsem="d_l%d" % hb)
                P.emit("pool", lambda e, hd=hd: e.dma_start(
                    out=cV[:, :, 0:256], in_=cv_c[hd].rearrange("(t p) d -> p t d", p=128)),
                    writes=[t_h], dsem="d_c%d" % hb)

            def cktr(hd):
                hb = hd % 2
                for m in range(2):
                    for t in range(4):
                        P.emit("pe", lambda e, t=t, m=m: e.transpose(out=PS[7][:, t * 128:(t + 1) * 128], in_=ckf_[hb][:, m, t, :], identity=ident[:]),
                               reads=[t_ckf_[hb], t_const], writes=[PT[7]])
                    P.emit("dve", lambda e, m=m: e.tensor_copy(out=ckT_[hb][:, m, :], in_=PS[7][:]), reads=[PT[7]], writes=[t_ckT_[hb]])

            loads(0)
            cktr(0)
            for hd in range(8):
                hb = hd % 2
                kT1, q1T, kpT, ckT, V1, cV, Vp = kT1_[hb], q1T_[hb], kpT_[hb], ckT_[hb], V1_[hb], cV_[hb], Vp_[hb]
                cur["t_h"], cur["t_ckT"] = t_h_[hb], t_ckT_[hb]
                if hd + 1 < 8:
                    loads(hd + 1)
                for s_ in range(2):
                    kts = [((lambda m, s_=s_, kt=kt, kpT=kpT: kpT[:, m, s_ * 256 + kt * 128:s_ * 256 + (kt + 1) * 128]), Vp[:, s_ * 2 + kt, :])
                           for kt in range(2)]
                    block(lambda m, s_=s_, q1T=q1T: q1T[:, m, s_ * 256:(s_ + 1) * 256], kts, 256, s_ * 256)
                kts = [((lambda m, kt=kt, ckT=ckT: ckT[:, m, kt * 128:(kt + 1) * 128]), cV[:, kt, :]) for kt in range(4)]
                kts += [((lambda m, kt=kt, kT1=kT1: kT1[:, m, kt * 128:(kt + 1) * 128]), V1[:, kt, :]) for kt in range(32)]
                for qb in range(4):
                    block(lambda m, qb=qb, q1T=q1T: q1T[:, m, 512 + qb * 512:512 + (qb + 1) * 512], kts, 512, 512 + qb * 512)
                    if qb == 2 and hd + 1 < 8:
                        cktr(hd + 1)
                flush()
                for c in range(2):
                    P.emit("sp", lambda e, hd=hd, c=c: e.dma_start(out=mixT_s[2 * hd + c], in_=oT1[:, c, :]),
                           reads=[t_oT], writes=[tk_mix], dsem="d_o")
        P.barrier()

    import os
    ph_env = os.environ.get("PHASES")
    phs = set(ph_env.split(",")) if ph_env else None

    def on(name):
        return phs is None or name in phs
    if on("f00"):
        ffn_phase(0, 0, 6)
    if on("even"):
        even_proj()
        even_attn()
        even_conv()
        wout_phase(0, a_w_out, mixT_s, tk_mix)
    if on("f01"):
        ffn_phase(0, 1, 5)
    if on("f10"):
        ffn_phase(1, 0, 5)
    if on("odd_setup"):
        odd_setup()
    if on("odd_proj"):
        odd_proj()
    if on("odd_cc"):
        odd_cc()
    if on("odd_attn"):
        odd_attn()
    if on("odd_wout"):
        wout_phase(1, c_w_out, mixT_s, tk_mix)
    if on("f11"):
        ffn_phase(1, 1, 5)
    final_phase()
    P.emit("sp", None)
    P.build()
    top.close()
    return nc, P


_CACHE = {}


def _build_nbias(rpb, half):
    base = 0 if half == 0 else 32
    out = np.full((8, 128, 5, 6, 128), -30000.0, np.float32)
    kc = np.arange(64)
    qc = np.arange(64)
    cs = np.clip(qc - 8, 0, 48)
    colmask = (kc[:, None] >= cs[None, :]) & (kc[:, None] < cs[None, :] + 16)
    cidx = np.clip(kc[:, None] - qc[None, :] + 15, 0, 30)
    for var, p in {0: 7, 1: 0, 2: 1, 3: 14, 4: 15}.items():
        ws = min(max(2 * p, 0), 28)
        for qr2 in range(2):
            r = 2 * p + qr2 + base
            rs = min(max(r - 4, 0), 56)
            for t in range(6):
                for kr2 in range(2):
                    kr = ws + 2 * t + kr2 - 4 + base
                    if rs <= kr < rs + 8:
                        vals = rpb[:, kr - r + 7][:, cidx]
                        out[:, kr2 * 64:(kr2 + 1) * 64, var, t, qr2 * 64:(qr2 + 1) * 64] = np.where(colmask[None], vals, -30000.0)
    return out.reshape(8, 128, 5 * 768)


def _rope_tables(own0):
    t = own0 + np.arange(2048)
    inv = (np.float32(10000.0) ** (-np.arange(32, dtype=np.float32) / np.float32(32))).astype(np.float32)
    row = (t // 64).astype(np.float32)
    col = (t % 64).astype(np.float32)
    cosT = np.zeros((128, 2048), np.float32)
    sinT = np.zeros((128, 2048), np.float32)
    for d in range(128):
        ang = ((row if d < 64 else col) * inv[d % 32]).astype(np.float32)
        cosT[d] = np.cos(ang)
        sn = np.sin(ang)
        sinT[d] = -sn if (d % 64) < 32 else sn
    return cosT, sinT


def _perm():
    p = np.zeros((128, 128), np.float32)
    for m in range(128):
        p[m + 32 if (m % 64) < 32 else m - 32, m] = 1.0
    return p


def _core_inputs(c, I):
    b, half = c // 2, c % 2
    own0 = half * 2048
    xs = I["x_sample"][b]
    xin = np.zeros((NTA, D), np.float32)
    xin[0:512] = I["x_prompt"][2 * c:2 * c + 2].reshape(512, D)
    xin[512:2560] = xs[own0:own0 + 2048]
    if half == 1:
        xin[2560:2816] = xs[own0 - 256:own0]
    else:
        xin[2816:3072] = xs[own0 + 2048:own0 + 2304]
    cT = np.zeros((128, 16, 2), np.float32)
    cT[:, :, 0] = I["c_ctx"].reshape(16, 128).T
    cT[:, :, 1] = I["c"][b].reshape(16, 128).T
    ngT = I["norm_g"].reshape(6, 16, 128).transpose(2, 0, 1).reshape(128, 96)
    zmask = np.zeros((1, 512), np.float32)
    if half == 0:
        zmask[0, 256:] = 1.0
    else:
        zmask[0, :256] = 1.0
    convp = np.concatenate([I["b_dw_b"][0].reshape(8, 128).T, I["b_ln_g"][0].reshape(8, 128).T,
                            I["b_ln_b"][0].reshape(8, 128).T], axis=1)
    dwT = I["b_dw_w"][0].reshape(31, 8, 128).transpose(2, 1, 0).reshape(128, 248)
    wmh = np.ascontiguousarray(I["w_mod"].reshape(2, D, 9, 2, 1024)[:, :, :, half, :]).reshape(2, D, 9 * 1024)
    bmh = np.ascontiguousarray(I["b_mod"].reshape(2, 9, 2, 1024)[:, :, half, :]).reshape(2, 9 * 1024)
    m = dict(xin=xin, cT=cT.reshape(128, 32), ngT=np.ascontiguousarray(ngT), w_mod_h=wmh, b_mod_h=bmh,
             nbias=_build_nbias(I["a_rpb"][0], half), zmask=zmask, convp=np.ascontiguousarray(convp),
             dwT=np.ascontiguousarray(dwT), cache_a_k=np.ascontiguousarray(I["cache_a_k"][b, 0]),
             cache_a_v=np.ascontiguousarray(I["cache_a_v"][b, 0]),
             cache_c_k=np.ascontiguousarray(I["cache_c_k"][b, 0]), cache_c_v=np.ascontiguousarray(I["cache_c_v"][b, 0]),
             lamb=np.ascontiguousarray(I["c_lambda"][0].reshape(1, 512)), subg=np.ascontiguousarray(I["c_subln_g"][0].reshape(1, 256)),
             ropec=_rope_tables(own0)[0], ropes=_rope_tables(own0)[1], perm_in=_perm(),
             fgT=np.ascontiguousarray(I["final_g"].reshape(16, 128).T),
             ident_in=np.eye(128, dtype=np.float32))
    return m


def kernel(**inputs):
    I = {k: np.asarray(v) for k, v in inputs.items()}
    if "nc" not in _CACHE:
        _CACHE["nc"] = build_program()[0]
    nc = _CACHE["nc"]
    shared = dict(ffn_w1=I["ffn_w1"], ffn_w3=I["ffn_w3"], ffn_w2=I["ffn_w2"],
                  a_w_in=I["a_w_in"][0], a_w_out=I["a_w_out"][0], c_w_in=I["c_w_in"][0], c_w_out=I["c_w_out"][0])
    in_maps = []
    for c in range(8):
        m = _core_inputs(c, I)
        m.update(shared)
        in_maps.append(m)
    res = run_bass_kernel_spmd(nc, in_maps, core_ids=list(range(8)))
    R = res.results
    y_prompt = np.concatenate([R[c]["y"][0:512].reshape(2, 256, D) for c in range(8)], axis=0)
    y_sample = np.stack([np.concatenate([R[2 * b]["y"][512:2560], R[2 * b + 1]["y"][512:2560]], axis=0)
                         for b in range(4)], axis=0)
    def heads(name, nh, dh):
        return np.concatenate([R[c][name].reshape(2, 256, nh, dh).transpose(0, 2, 1, 3) for c in range(8)], axis=0)[:, None]
    return (y_prompt, y_sample, heads("nak", 8, 128), heads("nav", 8, 128), heads("nck", 16, 128), heads("ncv", 8, 256))
```
